# Optimizing a Trainium2 kernel written in Bass

```python
import math
import jax
import jax.numpy as jnp
import numpy as np

D_MODEL = 1024
BATCH = 4
SEQ = 4096
DEPTH = 2

N_A_LAYERS = DEPTH // 2
N_B_LAYERS = DEPTH - N_A_LAYERS
HEAD_DIM = 64
N_FOX_HEADS = 12
FOX_WIDTH = N_FOX_HEADS * HEAD_DIM
N_DIFF_HEADS = 6
DIFF_QK_WIDTH = N_DIFF_HEADS * 2 * HEAD_DIM
DIFF_V_WIDTH = N_DIFF_HEADS * 2 * HEAD_DIM
N_MEM = 256
N_MEM_HEADS = 4
MEM_WIDTH = N_MEM_HEADS * HEAD_DIM
MIX_WIDTH = FOX_WIDTH + MEM_WIDTH
A_IN_WIDTH = 3 * FOX_WIDTH + N_FOX_HEADS + MEM_WIDTH
B_IN_WIDTH = DIFF_QK_WIDTH + MEM_WIDTH
SHARED_KV_WIDTH = DIFF_QK_WIDTH + DIFF_V_WIDTH
D_FF = -(-8 * D_MODEL // (3 * 256)) * 256
BLOCK_Q = 128
ROPE_THETA = 10000.0
NORM_EPS = 1e-6

kernel_name = "yoco_fox_diffattn_hybrid"


def rmsnorm(x, g):
    xf = x.astype(jnp.float32)
    y = xf * jax.lax.rsqrt(jnp.mean(xf * xf, axis=-1, keepdims=True) + NORM_EPS)
    return (y * g.astype(jnp.float32)).astype(x.dtype)


def split_heads(t, n_heads):
    B, S, W = t.shape
    return t.reshape(B, S, n_heads, W // n_heads).transpose(0, 2, 1, 3)


def merge_heads(t):
    B, H, S, Dh = t.shape
    return t.transpose(0, 2, 1, 3).reshape(B, S, H * Dh)


def rope(t):
    S, Dh = t.shape[-2], t.shape[-1]
    half = Dh // 2
    inv_freq = jnp.power(ROPE_THETA, -jnp.arange(half, dtype=jnp.float32) * (2.0 / Dh))
    ang = jnp.arange(S, dtype=jnp.float32)[:, None] * inv_freq[None, :]
    cos, sin = jnp.cos(ang), jnp.sin(ang)
    tf = t.astype(jnp.float32)
    t1, t2 = tf[..., :half], tf[..., half:]
    return jnp.concatenate([t1 * cos - t2 * sin, t1 * sin + t2 * cos], axis=-1).astype(t.dtype)


def causal_mask(q0, k_len):
    qi = q0 + jnp.arange(BLOCK_Q)[:, None]
    ki = jnp.arange(k_len)[None, :]
    return qi >= ki


def fox_attention(q, k, v, log_f):
    S, Dh = q.shape[2], q.shape[3]
    scale = Dh ** -0.5
    c = jnp.cumsum(log_f, axis=-1)
    outs = []
    for blk in range(S // BLOCK_Q):
        q0 = blk * BLOCK_Q
        q_end = q0 + BLOCK_Q
        s = jnp.einsum('bhqd,bhkd->bhqk', q[:, :, q0:q_end], k[:, :, :q_end]).astype(jnp.float32) * scale
        s = s + (c[:, :, q0:q_end, None] - c[:, :, None, :q_end])
        p = jax.nn.softmax(jnp.where(causal_mask(q0, q_end), s, -jnp.inf), axis=-1)
        outs.append(jnp.einsum('bhqk,bhkd->bhqd', p.astype(v.dtype), v[:, :, :q_end]))
    return jnp.concatenate(outs, axis=2)


def diff_attention(q1, q2, k1, k2, v, lam):
    S, Dh = q1.shape[2], q1.shape[3]
    scale = Dh ** -0.5
    outs = []
    for blk in range(S // BLOCK_Q):
        q0 = blk * BLOCK_Q
        q_end = q0 + BLOCK_Q
        mask = causal_mask(q0, q_end)
        s1 = jnp.einsum('bhqd,bhkd->bhqk', q1[:, :, q0:q_end], k1[:, :, :q_end]).astype(jnp.float32) * scale
        s2 = jnp.einsum('bhqd,bhkd->bhqk', q2[:, :, q0:q_end], k2[:, :, :q_end]).astype(jnp.float32) * scale
        p = (jax.nn.softmax(jnp.where(mask, s1, -jnp.inf), axis=-1)
             - lam * jax.nn.softmax(jnp.where(mask, s2, -jnp.inf), axis=-1))
        outs.append(jnp.einsum('bhqk,bhkd->bhqd', p.astype(v.dtype), v[:, :, :q_end]))
    return jnp.concatenate(outs, axis=2)


def memory_kv(mem, g, w_kv):
    kv = rmsnorm(mem, g) @ w_kv
    k, v = jnp.split(kv, 2, axis=-1)
    return split_heads(k, N_MEM_HEADS), split_heads(v, N_MEM_HEADS)


def memory_attention(mq, mem_k, mem_v):
    scale = mq.shape[-1] ** -0.5
    s = jnp.einsum('bhqd,bhmd->bhqm', mq, mem_k).astype(jnp.float32) * scale
    p = jax.nn.softmax(s, axis=-1)
    return jnp.einsum('bhqm,bhmd->bhqd', p.astype(mem_v.dtype), mem_v)


def fox_mixer(h, mem_k, mem_v, w_in, b_f):
    proj = h @ w_in
    q, k, v, f_logit, mq = jnp.split(
        proj, [FOX_WIDTH, 2 * FOX_WIDTH, 3 * FOX_WIDTH, 3 * FOX_WIDTH + N_FOX_HEADS], axis=-1)
    log_f = jax.nn.log_sigmoid(f_logit.astype(jnp.float32) + b_f.astype(jnp.float32)).transpose(0, 2, 1)
    y_fox = fox_attention(split_heads(q, N_FOX_HEADS), split_heads(k, N_FOX_HEADS),
                          split_heads(v, N_FOX_HEADS), log_f)
    y_mem = memory_attention(split_heads(mq, N_MEM_HEADS), mem_k, mem_v)
    return jnp.concatenate([merge_heads(y_fox), merge_heads(y_mem)], axis=-1)


def shared_kv(x, g, w):
    B, S, _ = x.shape
    kv = rmsnorm(x, g) @ w
    k, v = jnp.split(kv, [DIFF_QK_WIDTH], axis=-1)
    k = k.reshape(B, S, N_DIFF_HEADS, 2, HEAD_DIM).transpose(0, 2, 3, 1, 4)
    return rope(k[:, :, 0]), rope(k[:, :, 1]), split_heads(v, N_DIFF_HEADS)


def diff_mixer(h, k1, k2, v, mem_k, mem_v, w_in, lq1, lk1, lq2, lk2, subln_g, lambda_init):
    B, S, _ = h.shape
    proj = h @ w_in
    q, mq = jnp.split(proj, [DIFF_QK_WIDTH], axis=-1)
    q = q.reshape(B, S, N_DIFF_HEADS, 2, HEAD_DIM).transpose(0, 2, 3, 1, 4)
    q1, q2 = rope(q[:, :, 0]), rope(q[:, :, 1])
    lam = (jnp.exp(jnp.sum(lq1.astype(jnp.float32) * lk1.astype(jnp.float32)))
           - jnp.exp(jnp.sum(lq2.astype(jnp.float32) * lk2.astype(jnp.float32))) + lambda_init)
    y = diff_attention(q1, q2, k1, k2, v, lam)
    y = rmsnorm(y, subln_g) * (1.0 - lambda_init)
    y_mem = memory_attention(split_heads(mq, N_MEM_HEADS), mem_k, mem_v)
    return jnp.concatenate([merge_heads(y), merge_heads(y_mem)], axis=-1)


def swiglu(h, w_gate_up, w_down):
    g, u = jnp.split(h @ w_gate_up, 2, axis=-1)
    return (jax.nn.silu(g) * u) @ w_down


def setup_inputs(seed: int = 0) -> dict:
    key = jax.random.key(seed)
    ks = jax.random.split(key, 20)

    def nrm(k, shape, fan_in):
        return jax.random.normal(k, shape, jnp.float32) * (fan_in ** -0.5)

    def gain(k, shape):
        return 1.0 + 0.02 * jax.random.normal(k, shape, jnp.float32)

    return {
        "x": jax.random.normal(ks[0], (BATCH, SEQ, D_MODEL), jnp.float32),
        "mem": jax.random.normal(ks[1], (BATCH, N_MEM, D_MODEL), jnp.float32),
        "attn_norm_g": gain(ks[2], (DEPTH, D_MODEL)),
        "mem_norm_g": gain(ks[3], (DEPTH, D_MODEL)),
        "w_mem_kv": nrm(ks[4], (DEPTH, D_MODEL, 2 * MEM_WIDTH), D_MODEL),
        "w_out": nrm(ks[5], (DEPTH, MIX_WIDTH, D_MODEL), MIX_WIDTH),
        "ffn_norm_g": gain(ks[6], (DEPTH, D_MODEL)),
        "w_gate_up": nrm(ks[7], (DEPTH, D_MODEL, 2 * D_FF), D_MODEL),
        "w_down": nrm(ks[8], (DEPTH, D_FF, D_MODEL), D_FF),
        "a_w_in": nrm(ks[9], (N_A_LAYERS, D_MODEL, A_IN_WIDTH), D_MODEL),
        "a_b_f": jax.random.uniform(ks[10], (N_A_LAYERS, N_FOX_HEADS), jnp.float32, 1.0, 4.0),
        "b_w_in": nrm(ks[11], (N_B_LAYERS, D_MODEL, B_IN_WIDTH), D_MODEL),
        "b_lambda_q1": 0.1 * jax.random.normal(ks[12], (N_B_LAYERS, HEAD_DIM), jnp.float32),
        "b_lambda_k1": 0.1 * jax.random.normal(ks[13], (N_B_LAYERS, HEAD_DIM), jnp.float32),
        "b_lambda_q2": 0.1 * jax.random.normal(ks[14], (N_B_LAYERS, HEAD_DIM), jnp.float32),
        "b_lambda_k2": 0.1 * jax.random.normal(ks[15], (N_B_LAYERS, HEAD_DIM), jnp.float32),
        "b_subln_g": gain(ks[16], (N_B_LAYERS, 2 * HEAD_DIM)),
        "kv_norm_g": gain(ks[17], (D_MODEL,)),
        "w_kv_shared": nrm(ks[18], (D_MODEL, SHARED_KV_WIDTH), D_MODEL),
        "final_norm_g": gain(ks[19], (D_MODEL,)),
    }


def reference(x, mem, attn_norm_g, mem_norm_g, w_mem_kv, w_out, ffn_norm_g, w_gate_up, w_down,
              a_w_in, a_b_f, b_w_in, b_lambda_q1, b_lambda_k1, b_lambda_q2, b_lambda_k2,
              b_subln_g, kv_norm_g, w_kv_shared, final_norm_g):
    k1 = k2 = v_shared = None
    for layer in range(DEPTH):
        mem_k, mem_v = memory_kv(mem, mem_norm_g[layer], w_mem_kv[layer])
        if layer < N_A_LAYERS:
            h = rmsnorm(x, attn_norm_g[layer])
            mix = fox_mixer(h, mem_k, mem_v, a_w_in[layer], a_b_f[layer])
        else:
            if layer == N_A_LAYERS:
                k1, k2, v_shared = shared_kv(x, kv_norm_g, w_kv_shared)
            j = layer - N_A_LAYERS
            lambda_init = 0.8 - 0.6 * math.exp(-0.3 * layer)
            h = rmsnorm(x, attn_norm_g[layer])
            mix = diff_mixer(h, k1, k2, v_shared, mem_k, mem_v, b_w_in[j],
                             b_lambda_q1[j], b_lambda_k1[j], b_lambda_q2[j], b_lambda_k2[j],
                             b_subln_g[j], lambda_init)
        x = x + mix @ w_out[layer]
        x = x + swiglu(rmsnorm(x, ffn_norm_g[layer]), w_gate_up[layer], w_down[layer])
    return rmsnorm(x, final_norm_g)
```

```python
import math
from contextlib import ExitStack

import ml_dtypes
import numpy as np

import concourse.bass as bass
import concourse.mybir as mybir
from concourse.bass_utils import run_bass_kernel_spmd

F32 = mybir.dt.float32
BF16 = mybir.dt.bfloat16
AF = mybir.ActivationFunctionType
ALU = mybir.AluOpType

D = 1024
S = 4096
NB = 16
T = NB * 128
DFF = 2816
NFC = DFF // 128
EPS = 1e-6
LAM_INIT = 0.8 - 0.6 * math.exp(-0.3 * 1)

C_G_ATT0, C_G_ATT1, C_G_FFN0, C_G_FFN1, C_G_KV, C_G_MEM0, C_G_MEM1 = [8 * i for i in range(7)]
C_BF = 56
C_PAR = 68
C_LAM = 69
C_SUBG = C_LAM + 256
C_FING = C_SUBG + 128
C_ONE = C_FING + 1024
C_EPS = C_ONE + 1
NCST = C_EPS + 3


def seq_block(p, n):
    m, e = divmod(n, 2)
    return 4 * m + ((0, 3)[e] if p == 0 else (1, 2)[e])


def JN(n):
    m, e = divmod(n, 2)
    return 4 * m + (2 if e == 0 else 4)


def gidx(j):
    m, q = divmod(j, 4)
    r = 0 if q in (0, 3) else 1
    e = 0 if q in (0, 1) else 1
    return r, 2 * m + e


def MM(out, lhsT, rhs, start=True, stop=True, skip=False):
    if skip:
        return lambda e: e.matmul(out, lhsT, rhs, start=start, stop=stop, skip_group_check=True)
    return lambda e: e.matmul(out, lhsT, rhs, start=start, stop=stop)


def TR(out, in_, ident):
    return lambda e: e.transpose(out, in_, ident)


def ACT(out, in_, func, bias=None, scale=None, accum=None):
    def f(e):
        kw = {}
        if bias is not None:
            kw["bias"] = bias
        if scale is not None:
            kw["scale"] = scale
        if accum is not None:
            kw["accum_out"] = accum
        return e.activation(out, in_, func, **kw)
    return f


def TT(out, a, b, op):
    return lambda e: e.tensor_tensor(out, a, b, op)


def TS(out, a, s1, s2, op0, op1=None):
    if op1 is None:
        return lambda e: e.tensor_scalar(out, a, s1, None, op0)
    return lambda e: e.tensor_scalar(out, a, s1, s2, op0, op1)


def STT(out, in0, scalar, in1, op0, op1):
    return lambda e: e.scalar_tensor_tensor(out, in0, scalar, in1, op0, op1)


def CP(out, in_):
    return lambda e: e.tensor_copy(out, in_)


def MS(ap, val):
    return lambda e: e.memset(ap, val)


def RCP(out, in_):
    return lambda e: e.reciprocal(out, in_)


class Buf:
    __slots__ = ("name", "w", "r", "dsem", "dcnt")

    def __init__(self, name):
        self.name = name
        self.w = {}
        self.r = {}
        self.dsem = None
        self.dcnt = 0


class Sched:
    CE = ("pe", "act", "dve", "pool")
    ALL = ("pe", "act", "dve", "pool", "sp")

    def __init__(self, nc, stack):
        self.nc = nc
        self.stack = stack
        self.prog = {e: [] for e in self.ALL}
        self.cnt = {e: 0 for e in self.CE}
        self.csem = {e: self.newsem("c_" + e) for e in self.CE}
        self.waited = {e: {} for e in self.ALL}
        self.semobj = {}
        self.dbufs = []
        self.nsem = 4

    def newsem(self, name):
        return self.stack.enter_context(self.nc.semaphore(name))

    def _wait(self, e, key, sem, val):
        if self.waited[e].get(key, 0) >= val:
            return
        self.waited[e][key] = val
        self.prog[e].append(lambda eng: eng.wait_ge(sem, val))

    def _deps(self, e, reads, writes):
        deps = {}
        for b in reads:
            for k, v in b.w.items():
                if deps.get(k, (None, 0))[1] < v[1]:
                    deps[k] = v
        for b in writes:
            for dd in (b.w, b.r):
                for k, v in dd.items():
                    if deps.get(k, (None, 0))[1] < v[1]:
                        deps[k] = v
        for k, (sem, val) in deps.items():
            self._wait(e, k, sem, val)

    def _record(self, key, sem, val, reads, writes):
        for b in reads:
            if b.r.get(key, (None, 0))[1] < val:
                b.r[key] = (sem, val)
        for b in writes:
            b.w = {key: (sem, val)}
            b.r = {}

    def op(self, e, fns, reads=(), writes=()):
        if not isinstance(fns, (list, tuple)):
            fns = [fns]
        self._deps(e, reads, writes)
        self.cnt[e] += 1
        sem = self.csem[e]
        for f in fns[:-1]:
            self.prog[e].append(f)
        last = fns[-1]
        self.prog[e].append(lambda eng: last(eng).then_inc(sem, 1))
        self._record("c_" + e, sem, self.cnt[e], reads, writes)

    def dma(self, q, out, in_, sbuf, reads=(), writes=()):
        own = "d_" + sbuf.name
        saved = None
        if sbuf in writes and sbuf not in reads and own in sbuf.w:
            saved = sbuf.w.pop(own)
        self._deps(q, reads, writes)
        if saved is not None:
            sbuf.w[own] = saved
        if sbuf.dsem is None:
            sbuf.dsem = self.newsem("d_" + sbuf.name)
            self.nsem += 1
            self.dbufs.append(sbuf)
        sbuf.dcnt += 16
        sem = sbuf.dsem
        self.prog[q].append(lambda eng: eng.dma_start(out=out, in_=in_).then_inc(sem, 16))
        self._record("d_" + sbuf.name, sem, sbuf.dcnt, reads, writes)

    def collective(self, ins, outs, groups, name, reads, writes, after=()):
        self._deps("pool", reads, list(writes) + list(after))
        sem = self.newsem("cc_" + name)
        self.nsem += 1
        self.prog["pool"].append(
            lambda eng: eng.collective_compute("AllGather", ALU.bypass, replica_groups=groups,
                                               ins=[ins.opt()], outs=[outs.opt()]).then_inc(sem))
        self._record("cc_" + name, sem, 1, reads, writes)

    def barrier(self):
        for e in self.ALL:
            for c in self.CE:
                if self.cnt[c]:
                    self._wait(e, "c_" + c, self.csem[c], self.cnt[c])
            for b in self.dbufs:
                self._wait(e, "d_" + b.name, b.dsem, b.dcnt)

    def finish(self):
        for b in self.dbufs:
            self._wait("sp", "d_" + b.name, b.dsem, b.dcnt)

    def emit(self, block):
        prog = self.prog

        @block.tensor
        def _(eng):
            for f in prog["pe"]:
                f(eng)

        @block.scalar
        def _(eng):
            for f in prog["act"]:
                f(eng)

        @block.vector
        def _(eng):
            for f in prog["dve"]:
                f(eng)

        @block.gpsimd
        def _(eng):
            for f in prog["pool"]:
                f(eng)

        @block.sync
        def _(eng):
            for f in prog["sp"]:
                f(eng)


class Builder:
    def __init__(self, mode="ALL", dbg=None):
        self.mode = mode
        self.dbg = dbg
        self.nc = bass.Bass("TRN2", target_bir_lowering=False)
        self.io = {}

    def din(self, name, shape, dt=F32):
        t = self.nc.dram_tensor(name, list(shape), dt, kind="ExternalInput").ap()
        self.io[name] = t
        return t

    def dout(self, name, shape, dt=F32):
        t = self.nc.dram_tensor(name, list(shape), dt, kind="ExternalOutput").ap()
        self.io[name] = t
        return t

    def dscr(self, name, shape, dt):
        return self.nc.dram_tensor(name, list(shape), dt).ap()

    def exch(self, name, shape_own, dt, own_kind, full_kind):
        mk = {"scr": self.dscr, "in": self.din, "out": self.dout}
        own = mk[own_kind](name + "_b", shape_own, dt) if own_kind else None
        full = mk[full_kind](name + "_g", [2 * shape_own[0], shape_own[1]], dt) if full_kind else None
        return own, full

    def aview(self, off, dt, shape):
        esz = 4 if dt == F32 else 2
        n = int(np.prod(shape)) * esz
        assert off % 4 == 0 and n % 4 == 0
        assert off + n <= self.ARENA, (off, n, self.ARENA)
        ap = self.arena[:, off // 4:(off + n) // 4]
        if dt != F32:
            ap = ap.bitcast(dt)
        if len(shape) == 2:
            ap = ap.rearrange("p (a b) -> p a b", a=shape[0])
        elif len(shape) == 3:
            ap = ap.rearrange("p (a b c) -> p a b c", a=shape[0], b=shape[1])
        return ap

    def build(self):
        nc = self.nc
        mode = self.mode
        fused = mode == "ALL"
        P1 = mode in ("ALL", "P1")
        P2 = mode in ("ALL", "P2")
        P3 = mode in ("ALL", "P3")

        self.x_in = self.din("x_own", [T, D])
        self.mem_in = self.din("mem_b", [256, D])
        self.cst_in = self.din("cst", [128, NCST])
        self.cbf_in = self.din("cbf", [128, 5 * 128], BF16)
        self.cf_in = self.din("cf32", [128, 256])
        self.cos_in = self.din("cosT", [128, T])
        self.sin_in = self.din("sinT", [128, T])
        self.w_a_in = self.din("a_w_in", [D, 2572])
        self.w_b_in = self.din("b_w_in", [D, D])
        self.w_out = self.din("w_out", [2 * D, D])
        self.w_gu = self.din("w_gate_up", [2 * D, 2 * DFF])
        self.w_dn = self.din("w_down", [2 * DFF, D])
        self.w_mem = self.din("w_mem_kv", [2 * D, 512])
        self.w_kvs = self.din("w_kv_shared", [D, 1536])

        def kinds(prod_phase, cons_phase):
            if fused:
                return "scr", "scr"
            return ("out" if mode == prod_phase else None), ("in" if mode == cons_phase else None)

        ko, kf = kinds("P1", "P2")
        self.kT0_b, self.kT0_g = zip(*[self.exch(f"kT0t{t}", [768, 1024], BF16, ko, kf) for t in range(2)])
        self.v0_b, self.v0_g = zip(*[self.exch(f"v0t{t}", [1024, 780], BF16, ko, kf) for t in range(2)])
        self.lf_b, self.lf_g = self.exch("lf0", [T, 12], F32, ko, kf)
        ko, kf = kinds("P2", "P3")
        self.kT1_b, self.kT1_g = zip(*[self.exch(f"kT1t{t}", [768, 1024], BF16, ko, kf) for t in range(2)])
        self.v1_b, self.v1_g = zip(*[self.exch(f"v1t{t}", [1024, 774], BF16, ko, kf) for t in range(2)])
        if fused or mode == "P3":
            self.out_d = self.dout("out_own", [T, D])
        if mode in ("P1", "P2"):
            self.xmid_d = self.dout("x_mid", [T, D]) if mode == "P2" else None
        self.dbg_out = {}
        if self.dbg in ('L0', 'op0', 'ffn0', 'kv10'):
            self.dbg_d = self.dout('dbg', [T, D])
        if self.dbg in ('ct', 'mem', 'qp'):
            self.dbg_s = self.dout('dbg_s', [128, 8192])
        if self.dbg == 'attn0':
            self.dbg_m = self.dout('dbg_m', [128, 8 * 1024])

        self.ARENA = 111 * 1024
        with ExitStack() as st:
            self.st = st
            E = st.enter_context
            self.x_sb = E(nc.sbuf_tensor("x_sb", [128, NB, D], F32))
            self.arena = E(nc.sbuf_tensor("arena", [128, self.ARENA // 4], F32))
            self.cst = E(nc.sbuf_tensor("cst_sb", [128, NCST], F32))
            self.cbf = E(nc.sbuf_tensor("cbf_sb", [128, 5 * 128], BF16))
            self.cf = E(nc.sbuf_tensor("cf_sb", [128, 256], F32))
            self.htm = [E(nc.sbuf_tensor(f"htm{i}", [128, D], BF16)) for i in range(2)]
            self.pT = [E(nc.sbuf_tensor(f"pT{i}", [128, 512], BF16)) for i in range(4)]
            self.stat = E(nc.sbuf_tensor("stat", [128, 64], F32))
            self.statx = E(nc.sbuf_tensor("statx", [128, 32], F32))
            self.b_statx = Buf("statx")
            self.ctab = E(nc.sbuf_tensor("ctab", [128, 32 * 12 + 16 * 12], F32))
            self.wdn_x = [E(nc.sbuf_tensor(f"wdnx{i}", [128, 1024], BF16)) for i in range(2)]
            self.biasG = [E(nc.sbuf_tensor(f"biasG{i}", [128, 2, 4, 32], F32)) for i in range(2)]
            self.rden = [E(nc.sbuf_tensor(f"rden{i}", [128, 8], F32)) for i in range(4)]
            self.mixtm = [E(nc.sbuf_tensor(f"mixtm{i}", [128, 128], BF16)) for i in range(4)]
            self.ytmp = [E(nc.sbuf_tensor(f"ytmp{i}", [128, 2, 128], F32)) for i in range(2)]
            self.memKT = E(nc.sbuf_tensor("memKT", [128, 2, 256], BF16))
            self.memV = E(nc.sbuf_tensor("memV", [128, 2, 4, 65], BF16))
            self.ps = [E(nc.psum_tensor(f"ps{i}", [128, 512], F32)) for i in range(8)]
            self.S = Sched(nc, st)
            self.b_ps = [Buf(f"ps{i}") for i in range(8)]
            self.ring_i = 0
            self.b_x = [Buf(f"x{n}") for n in range(NB)]
            self.b_cst = Buf("cst")
            self.b_htm = [Buf("htm0"), Buf("htm1")]
            self.b_stat = Buf("stat")
            self.ident = self.cbf[:, 0:128]

            block = E(nc.Block())
            self.program(P1, P2, P3, fused)
            self.S.finish()
            self.S.emit(block)
        return nc

    def ring(self):
        i = self.ring_i
        self.ring_i = (i + 1) % 4
        return i

    def load_consts(self):
        S_ = self.S
        S_.dma("sp", self.cst[:], self.cst_in, self.b_cst, writes=[self.b_cst])
        self.b_cbf = Buf("cbf")
        S_.dma("sp", self.cbf[:], self.cbf_in, self.b_cbf, writes=[self.b_cbf])
        self.b_cf = Buf("cf")
        S_.dma("sp", self.cf[:], self.cf_in, self.b_cf, writes=[self.b_cf])

    def load_x(self, src, first=None, rest=False):
        rng = range(NB) if first is None else (range(first, NB) if rest else range(first))
        for n in rng:
            self.S.dma("sp", self.x_sb[:, n, :], src[n * 128:(n + 1) * 128, :], self.b_x[n], writes=[self.b_x[n]])

    def wload(self, dst_ap, src_ap, buf):
        self.S.dma("pool", dst_ap, src_ap, buf, writes=[buf])

    def x_sq(self, n):
        i = self._rms_i = (getattr(self, "_rms_i", 0) + 1) % 2
        self.S.op("act", ACT(self.htm[i][:], self.x_sb[:, n, :], AF.Square, accum=self.statx[:, n:n + 1]),
                  reads=[self.b_x[n]], writes=[self.b_htm[i], self.b_statx])

    def x_rstd(self, tile):
        st = self.statx
        self.S.op("act", ACT(st[:, 16 + tile * 8:24 + tile * 8], st[:, tile * 8:tile * 8 + 8], AF.Sqrt,
                             bias=self.cst[:, C_EPS:C_EPS + 1], scale=1.0 / D), reads=[self.b_cst], writes=[self.b_statx])
        self.S.op("dve", RCP(st[:, 16 + tile * 8:24 + tile * 8], st[:, 16 + tile * 8:24 + tile * 8]), writes=[self.b_statx])

    def rms_T_multi(self, srcs, gcol, dstT, dst_buf, rstd=None):
        S_ = self.S
        nb = len(srcs)
        st = self.stat
        bst = self.b_rst[0]
        if rstd is None:
            for k, (ap, buf) in enumerate(srcs):
                junk = dstT[:, :, k * 128:(k + 1) * 128]
                S_.op("act", ACT(junk, ap.rearrange("p (c f) -> p c f", c=8), AF.Square, accum=st[:, 32 + k:33 + k]),
                      reads=[buf], writes=[dst_buf, bst])
            S_.op("act", ACT(st[:, 40:40 + nb], st[:, 32:32 + nb], AF.Sqrt, bias=self.cst[:, C_EPS:C_EPS + 1], scale=1.0 / D),
                  reads=[self.b_cst], writes=[bst])
            S_.op("dve", RCP(st[:, 40:40 + nb], st[:, 40:40 + nb]), writes=[bst])
            rstd = [st[:, 40 + k:41 + k] for k in range(nb)]
        else:
            bst = self.b_statx
        gT = self.cst[:, gcol:gcol + 8].unsqueeze(2).broadcast_to([128, 8, 128])
        for k, (ap, buf) in enumerate(srcs):
            i = self._rms_i = (getattr(self, "_rms_i", 0) + 1) % 2
            S_.op("act", ACT(self.htm[i][:], ap, AF.Copy, scale=rstd[k]), reads=[buf, bst],
                  writes=[self.b_htm[i]])
            r = self.ring()
            pb = self.ps[r][:, 0:512].bitcast(BF16).rearrange("p (a b) -> p a b", a=8)
            S_.op("pe", [TR(pb[:, c, :], self.htm[i][:, c * 128:(c + 1) * 128], self.ident) for c in range(8)],
                  reads=[self.b_htm[i], self.b_cbf], writes=[self.b_ps[r]])
            S_.op("dve", TT(dstT[:, :, k * 128:(k + 1) * 128], pb, gT, ALU.mult),
                  reads=[self.b_ps[r], self.b_cst], writes=[dst_buf])

    def proj_fm(self, wT, wbuf, c0, hT, hbufs, ntok, evac):
        for tt in range(ntok // 512):
            r = self.ring()
            self.S.op("pe", [MM(self.ps[r][:, :], wT[:, c, c0:c0 + 128], hT[:, c, tt * 512:(tt + 1) * 512],
                                start=(c == 0), stop=(c == 7)) for c in range(8)],
                      reads=[wbuf, hbufs[tt // 2] if len(hbufs) * 1024 >= ntok and len(hbufs) > 1 else hbufs[0]],
                      writes=[self.b_ps[r]])
            evac(tt, self.ps[r], self.b_ps[r])

    def program(self, P1, P2, P3, fused):
        S_ = self.S
        nc = self.nc
        self.b_rst = [Buf("rst0"), Buf("rst1")]
        self.load_consts()
        if not P2:
            self.load_x(self.x_in)
        groups = [[0, 1], [2, 3], [4, 5], [6, 7]]

        A_HT = 0
        A_QT = 16 * 1024
        A_MQT = 28 * 1024
        A_KV = 32 * 1024
        A_WS = 66 * 1024
        A_WO = 74 * 1024
        A_STG = 90 * 1024
        A_ACT = 16 * 1024
        A_WGU = 99 * 1024
        A_WDN = 107 * 1024
        A_COS = 32 * 1024
        A_WM = 74 * 1024

        hT = self.aview(A_HT, BF16, [8, 1024])
        b_hT = [Buf("hT0")]
        qT = self.aview(A_QT, BF16, [6, 1024])
        mqT = self.aview(A_MQT, BF16, [2, 1024])
        b_qT = Buf("qT")
        b_mqT = Buf("mqT")
        ws = [self.aview(A_WS + i * 4096, BF16, [8, 256]) for i in range(2)] + \
             [self.aview(A_WGU + i * 4096, BF16, [8, 256]) for i in range(2)]
        b_ws = [Buf("ws0"), Buf("ws1"), Buf("ws2"), Buf("ws3")]
        self._ws_i = 0
        wo = self.aview(A_WO, BF16, [8, 1024])
        b_wo = Buf("wo")

        def ws_next():
            i = self._ws_i
            self._ws_i = (i + 1) % 4
            return ws[i], b_ws[i]

        self._ws128_i = 0

        def ws128_next():
            i = self._ws128_i
            self._ws128_i = (i + 1) % 6
            if i < 4:
                return ws[i], b_ws[i]
            return self.wdn_x[i - 4][:].rearrange("p (c f) -> p c f", c=8), b_wdn[2 + (i - 4)]

        def wview(w, row0, col0, ncol):
            return w[row0:row0 + D, :].rearrange("(c p) f -> p c f", p=128)[:, :, col0:col0 + ncol]

        kstg = [self.aview(A_STG + i * 2048, BF16, [1024]) for i in range(2)]
        b_kstg = [Buf("kstg0"), Buf("kstg1")]
        vstg = [self.aview(A_STG + 4096 + i * 1568, BF16, [784]) for i in range(2)]
        b_vstg = [Buf("vstg0"), Buf("vstg1")]
        lfstg = self.aview(A_STG + 8192, F32, [NB, 12])
        b_lfstg = Buf("lfstg")
        wf = self.aview(A_STG + 7232, BF16, [8, 12])
        b_wf = Buf("wf")
        zt = self.aview(A_STG + 7232 + 192, F32, [2, 12])
        b_zt = Buf("zt")

        def norm_tile(tile, gcol, dst, dbuf):
            self.rms_T_multi([(self.x_sb[:, tile * 8 + nb, :], self.b_x[tile * 8 + nb]) for nb in range(8)], gcol, dst, dbuf,
                             rstd=[self.statx[:, 16 + tile * 8 + nb:17 + tile * 8 + nb] for nb in range(8)])

        def kv_pass(layer, tile):
            gcol = C_G_ATT0 if layer == 0 else C_G_KV
            norm_tile(tile, gcol, hT, b_hT[0])
            w = self.w_a_in if layer == 0 else self.w_kvs
            kcol0 = 768 if layer == 0 else 0
            vcol0 = 1536 if layer == 0 else 768
            kT_b = self.kT0_b if layer == 0 else self.kT1_b
            v_b = self.v0_b if layer == 0 else self.v1_b
            VW = 65 if layer == 0 else 129
            nvh = 12 if layer == 0 else 6
            if layer == 1:
                cosT = self.aview(A_COS, F32, [1024])
                sinT = self.aview(A_COS + 4096, F32, [1024])
                b_cs = b_kT[0]
                S_.dma("sp", cosT, self.cos_in[:, tile * 1024:(tile + 1) * 1024], b_cs, writes=[b_cs])
                S_.dma("sp", sinT, self.sin_in[:, tile * 1024:(tile + 1) * 1024], b_cs, writes=[b_cs])
            for oc in range(6):
                if layer == 0:
                    wt, wb = ws128_next()
                    self.wload(wt[:, :, 0:128], wview(w, 0, kcol0 + oc * 128, 128), wb)
                    i = oc % 2

                    def evac(tt, ps, pbuf, i=i):
                        S_.op("act", ACT(kstg[i][:, tt * 512:(tt + 1) * 512], ps[:, :], AF.Copy),
                              reads=[pbuf], writes=[b_kstg[i]])
                    self.proj_fm(wt, wb, 0, hT, b_hT, 1024, evac)
                else:
                    self.rope_proj(w, kcol0 + oc * 128, hT, b_hT, cosT, sinT, b_cs, ws_next, wview,
                                   lambda tt, i=oc % 2: (kstg[i][:, tt * 512:(tt + 1) * 512], b_kstg[i]))
                    i = oc % 2
                S_.dma("sp", kT_b[tile][oc * 128:(oc + 1) * 128, :], kstg[i], b_kstg[i],
                       reads=[b_kstg[i]])
            wvs = []
            for vc in range(3):
                wt, wb = ws_next()
                self.wload(wt[:, :, 0:256], wview(w, 0, vcol0 + vc * 256, 256), wb)
                wvs.append((wt, wb))
            for nb in range(8):
                n = tile * 8 + nb
                i = nb % 2
                r = self.ring()
                r2 = self.ring()
                fns = []
                for vc in range(3):
                    pst = self.ps[r] if vc < 2 else self.ps[r2]
                    for c in range(8):
                        fns.append(MM(pst[:, (vc % 2) * 256:(vc % 2) * 256 + 256], hT[:, c, nb * 128:(nb + 1) * 128],
                                      wvs[vc][0][:, c, 0:256], start=(c == 0), stop=(c == 7)))
                S_.op("pe", fns, reads=[b_hT[0]] + [wb for _, wb in wvs], writes=[self.b_ps[r], self.b_ps[r2]])
                vv = vstg[i][:, 0:nvh * VW].rearrange("p (h w) -> p h w", h=nvh)
                if nb < 2:
                    S_.op("dve", MS(vstg[i][:, :], 1.0), writes=[b_vstg[i]])
                dv = VW - 1
                S_.op("act", ACT(vv[:, 0:(512 // dv), 0:dv], self.ps[r][:, :].rearrange("p (h w) -> p h w", w=dv), AF.Copy),
                      reads=[self.b_ps[r]], writes=[b_vstg[i]])
                S_.op("act", ACT(vv[:, (512 // dv):nvh, 0:dv], self.ps[r2][:, 0:256].rearrange("p (h w) -> p h w", w=dv), AF.Copy),
                      reads=[self.b_ps[r2]], writes=[b_vstg[i]])
                S_.dma("sp", v_b[tile][nb * 128:(nb + 1) * 128, :], vstg[i][:, 0:nvh * VW], b_vstg[i], reads=[b_vstg[i]])
            if layer == 0:
                self.wload(wf, wview(w, 0, 2304, 12), b_wf)
                for nb in range(8):
                    n = tile * 8 + nb
                    r = self.ring()
                    S_.op("pe", [MM(self.ps[r][:, 0:12], hT[:, c, nb * 128:(nb + 1) * 128], wf[:, c, :],
                                    start=(c == 0), stop=(c == 7)) for c in range(8)],
                          reads=[b_hT[0], b_wf], writes=[self.b_ps[r]])
                    z = zt[:, nb % 2, :]
                    S_.op("dve", TT(z, self.ps[r][:, 0:12], self.cst[:, C_BF:C_BF + 12], ALU.add),
                          reads=[self.b_ps[r], self.b_cst], writes=[b_zt])
                    S_.op("act", ACT(z, z, AF.Exp, scale=-1.0), reads=[b_zt], writes=[b_zt])
                    S_.op("act", ACT(lfstg[:, n, :], z, AF.Ln, bias=self.cst[:, C_ONE:C_ONE + 1]), reads=[b_zt, self.b_cst], writes=[b_lfstg])
                if tile == 1:
                    S_.dma("sp", self.lf_b.rearrange("(n p) h -> p n h", p=128), lfstg, b_lfstg, reads=[b_lfstg])

        kT_sb = [self.aview(A_KV + i * 8192, BF16, [2, 2048]) for i in range(2)]
        b_kT = [Buf("kT0"), Buf("kT1")]
        Vraw = [A_KV + 16384 + i * 8320 for i in range(2)]
        b_V = [Buf("V0"), Buf("V1")]
        actT = self.aview(A_ACT, BF16, [NFC, 1024])
        b_act = Buf("actT")
        wgu = [self.aview(A_WGU + i * 4096, BF16, [8, 2, 128]) for i in range(2)]
        b_wgu = [Buf("wgu0"), Buf("wgu1")]
        wdn = [self.aview(A_WDN + i * 2048, BF16, [1024]) for i in range(2)] + [t[:] for t in self.wdn_x]
        b_wdn = [Buf(f"wdn{i}") for i in range(len(wdn))]
        NWDN = len(wdn)
        b_pT = [[Buf(f"pT{s}_{k}") for k in range(4)] for s in range(4)]
        b_acc = [[Buf(f"acc{b}_{k}") for k in range(4)] for b in range(8)]
        b_mixtm = [Buf(f"mixtm{i}") for i in range(4)]
        b_biasG = [Buf("biasG0"), Buf("biasG1")]
        b_rden = [Buf(f"rden{i}") for i in range(4)]
        b_ytmp = [Buf("ytmp0"), Buf("ytmp1")]
        b_ctab = Buf("ctab")
        b_memK = Buf("memK")
        b_memV = Buf("memV")
        b_fence = Buf("fence")
        b_gate = [[[], []], [[], []]]
        b_gate_lf = []
        self.ropet = [self.aview(Vraw[i], F32, [2, 512]) for i in range(2)]
        self.b_ropet = b_V
        alias_Z = [b_qT, b_mqT, b_kT[0], b_kT[1], b_V[0], b_V[1], b_act]
        cnt = {"pT": 0, "mixtm": 0, "rden": 0, "biasG": 0, "ytmp": 0, "kv": 0, "wgu": 0, "wdn": 0, "grp": 0}

        def nxt(name, n):
            i = cnt[name]
            cnt[name] = (i + 1) % n
            return i

        def fence(bufs):
            S_.op("dve", MS(self.stat[:, 60:61], 0.0), writes=[b_fence] + list(bufs))

        ct = self.ctab
        CSEQ, CREF = 0, 384
        ctmp = self.aview(A_STG, F32, [33 * 12 + 8 * 12 + 32 * 12])
        PRE, TMPD, LFALL = 0, 396, 492

        def ctables():
            LF = ctmp[:, LFALL:LFALL + 384]
            S_.dma("sp", LF.rearrange("p (g h) -> p g h", h=12), self.lf_g.rearrange("(g p) h -> p g h", p=128),
                   b_ctab, reads=b_gate_lf, writes=[b_ctab] + b_kstg + b_vstg + [b_lfstg])
            r1 = self.ring()
            r2 = self.ring()
            S_.op("pe", MM(self.ps[r1][:, 0:384], self.cf[:, 0:128], LF), reads=[b_ctab, self.b_cf], writes=[self.b_ps[r1]])
            S_.op("pe", MM(self.ps[r2][:, 0:384], self.cf[:, 128:256], LF), reads=[b_ctab, self.b_cf], writes=[self.b_ps[r2]])
            S_.op("dve", MS(ctmp[:, PRE:PRE + 12], 0.0), writes=[b_ctab])
            for j in range(32):
                r, n = gidx(j)
                g = r * 16 + n
                S_.op("dve", TT(ctmp[:, PRE + (j + 1) * 12:PRE + (j + 2) * 12], ctmp[:, PRE + j * 12:PRE + (j + 1) * 12],
                                self.ps[r2][:, g * 12:(g + 1) * 12], ALU.add), reads=[self.b_ps[r2]], writes=[b_ctab])
            cw = self.ps[r1][:, 0:384].rearrange("p (r m e h) -> p r m e h", r=2, m=8, e=2)
            pre_v = ctmp[:, PRE:PRE + 384].rearrange("p (m q h) -> p m q h", m=8, q=4)
            cs_v = ct[:, CSEQ:CSEQ + 384].rearrange("p (m q h) -> p m q h", m=8, q=4)
            for q, (r, e) in enumerate(((0, 0), (1, 0), (1, 1), (0, 1))):
                S_.op("dve", TT(cs_v[:, :, q, :], pre_v[:, :, q, :], cw[:, r, :, e, :], ALU.add),
                      reads=[self.b_ps[r1]], writes=[b_ctab])
            inc_v = ctmp[:, PRE + 12:PRE + 12 + 384].rearrange("p (m q h) -> p m q h", m=8, q=4)
            ref_v = ct[:, CREF:CREF + 192].rearrange("p (m e h) -> p m e h", m=8, e=2)
            tmp_v = ctmp[:, TMPD:TMPD + 96].rearrange("p (m h) -> p m h", m=8)
            par = self.cst[:, C_PAR:C_PAR + 1]
            for e, (qa, qb) in enumerate(((0, 1), (3, 2))):
                S_.op("dve", TT(tmp_v, inc_v[:, :, qb, :], inc_v[:, :, qa, :], ALU.subtract), writes=[b_ctab])
                S_.op("dve", STT(ref_v[:, :, e, :], tmp_v, par, inc_v[:, :, qa, :], ALU.mult, ALU.add),
                      reads=[self.b_cst], writes=[b_ctab])

        def mem_kv(layer):
            mem_sb = self.aview(A_KV, F32, [2, 1024])
            hmT = self.aview(A_KV + 8192, BF16, [8, 256])
            wm = self.aview(A_WO, BF16, [8, 512])
            S_.dma("sp", mem_sb, self.mem_in.rearrange("(m p) d -> p m d", p=128), b_kT[0], writes=[b_kT[0]])
            self.wload(wm, wview(self.w_mem, layer * D, 0, 512), b_wo)
            self.rms_T_multi([(mem_sb[:, mb, :], b_kT[0]) for mb in range(2)], C_G_MEM0 if layer == 0 else C_G_MEM1, hmT, b_kT[1])
            for pm in range(2):
                r = self.ring()
                S_.op("pe", [MM(self.ps[r][:, 0:256], wm[:, c, pm * 128:(pm + 1) * 128], hmT[:, c, :],
                                start=(c == 0), stop=(c == 7)) for c in range(8)],
                      reads=[b_wo, b_kT[1]], writes=[self.b_ps[r]])
                S_.op("act", ACT(self.memKT[:, pm, :], self.ps[r][:, 0:256], AF.Copy), reads=[self.b_ps[r]], writes=[b_memK])
            S_.op("dve", MS(self.memV[:], 1.0), writes=[b_memV])
            for mb in range(2):
                r = self.ring()
                S_.op("pe", [MM(self.ps[r][:, 0:256], hmT[:, c, mb * 128:(mb + 1) * 128], wm[:, c, 256:512],
                                start=(c == 0), stop=(c == 7)) for c in range(8)],
                      reads=[b_wo, b_kT[1]], writes=[self.b_ps[r]])
                S_.op("act", ACT(self.memV[:, mb, :, 0:64], self.ps[r][:, 0:256].rearrange("p (h w) -> p h w", h=4), AF.Copy),
                      reads=[self.b_ps[r]], writes=[b_memV])

        def attend(steps, tile, units, Jof, masked, W, finish):
            for gl in range(2):
                ns = [tile * 8 + gl * 4 + nn for nn in range(4)]
                Jmax = Jof(ns[-1])
                gpar = nxt("grp", 2)
                if W == 65:
                    accap = lambda a, nn, gpar=gpar: (4 + a + 2 * gpar, nn, self.ps[4 + a + 2 * gpar][:, nn * 65:(nn + 1) * 65])
                else:
                    accap = lambda a, nn: (4 + 2 * a + nn // 2, nn % 2,
                                           self.ps[4 + 2 * a + nn // 2][:, (nn % 2) * 129:(nn % 2) * 129 + 129])
                ub = {}
                Jfar = max(0, Jof(ns[0]) - 2) if units[0].get("bias") else 0
                for j in range(Jmax):
                    nn0 = min(nn for nn in range(4) if Jof(ns[nn]) > j)
                    cn = 4 - nn0
                    for a, u in enumerate(units):
                        st = {}

                        def front(st=st, a=a, u=u, j=j, nn0=nn0, cn=cn, ns=ns, gl=gl, ub=ub, Jfar=Jfar):
                            if j == 0 and u.get("prep"):
                                ub[a] = u["prep"](gl, ns)
                                if Jfar > 0:
                                    if a == 0:
                                        ub["fi"] = nxt("rden", 4)
                                    fi = ub["fi"]
                                    h_ = u["head"]
                                    crv = ct[:, CREF:CREF + 192].rearrange("p (n h) -> p n h", h=12)[:, ns[1]:ns[3] + 1, h_]
                                    fv = self.rden[fi][:, a * 4 + 1:a * 4 + 4]
                                    S_.op("dve", TS(fv, crv, ct[:, CREF + ns[0] * 12 + h_:CREF + ns[0] * 12 + h_ + 1], None, ALU.subtract),
                                          reads=[b_ctab], writes=[b_rden[fi]])
                                    S_.op("act", ACT(fv, fv, AF.Exp, scale=-1.0), writes=[b_rden[fi]])
                            r = self.ring()
                            kap, kb = u["k"](j)
                            qap, qb = u["q"]((gl * 4 + nn0) * 128, cn * 128)
                            S_.op("pe", MM(self.ps[r][:, 0:cn * 128], kap, qap), reads=[kb, qb], writes=[self.b_ps[r]])
                            s_ = nxt("pT", 4)
                            st["s"] = s_
                            if u.get("bias") and j < Jfar:
                                bap, bb = u["bias"](ub[a], 0, j)
                                S_.op("act", ACT(self.pT[s_][:, 0:512], self.ps[r][:, 0:512], AF.Exp, bias=bap, scale=0.125),
                                      reads=[self.b_ps[r], bb], writes=b_pT[s_])
                            elif u.get("bias"):
                                for k in range(cn):
                                    bap, bb = u["bias"](ub[a], nn0 + k, j)
                                    S_.op("act", ACT(self.pT[s_][:, k * 128:(k + 1) * 128], self.ps[r][:, k * 128:(k + 1) * 128],
                                                     AF.Exp, bias=bap, scale=0.125),
                                          reads=[self.b_ps[r], bb], writes=[b_pT[s_][k]])
                            else:
                                S_.op("act", ACT(self.pT[s_][:, 0:cn * 128], self.ps[r][:, 0:cn * 128], AF.Exp, scale=0.125),
                                      reads=[self.b_ps[r]], writes=b_pT[s_][0:cn])
                            for k in range(cn):
                                n = ns[nn0 + k]
                                if masked and j >= Jof(n) - 2:
                                    mi = 1 + 2 * (n % 2) + (j - (Jof(n) - 2))
                                    S_.op("pool", TT(self.pT[s_][:, k * 128:(k + 1) * 128], self.pT[s_][:, k * 128:(k + 1) * 128],
                                                     self.cbf[:, mi * 128:(mi + 1) * 128], ALU.mult),
                                          reads=[self.b_cbf], writes=[b_pT[s_][k]])

                        def back(st=st, a=a, u=u, j=j, nn0=nn0, cn=cn, ns=ns, gl=gl, accap=accap, ub=ub, Jfar=Jfar,
                                 last=(j == Jmax - 1 and a == len(units) - 1)):
                            s_ = st["s"]
                            vap, vb = u["v"](j)
                            for k in range(cn):
                                nn = nn0 + k
                                bank, sub, aap = accap(a, nn)
                                S_.op("pe", MM(aap, self.pT[s_][:, k * 128:(k + 1) * 128], vap,
                                               start=(j == 0 and sub == 0), stop=(j == Jof(ns[nn]) - 1), skip=True),
                                      reads=[b_pT[s_][k], vb], writes=[b_acc[bank][sub]])
                            if Jfar > 0 and j == Jfar - 1:
                                fi = ub["fi"]
                                for nn in range(1, 4):
                                    bank, sub, aap = accap(a, nn)
                                    S_.op("dve", TS(aap, aap, self.rden[fi][:, a * 4 + nn:a * 4 + nn + 1], None, ALU.mult),
                                          reads=[b_rden[fi]], writes=b_acc[bank])
                            if last:
                                finish(gl, ns, accap)
                        steps.append((front, back))

        deferred = []
        cur_pair = [0]

        def put_mixT(slot, chunk, tokblk):
            deferred.append([cur_pair[0] + 2, lambda: put_mixT_now(slot, chunk, tokblk)])

        def run_deferred(flush=False):
            for item in list(deferred):
                if flush or item[0] <= cur_pair[0]:
                    item[1]()
                    deferred.remove(item)

        def put_mixT_now(slot, chunk, tokblk):
            r = self.ring()
            pb = self.ps[r][:, 0:64].bitcast(BF16)
            S_.op("pe", TR(pb, self.mixtm[slot][:], self.ident), reads=[b_mixtm[slot], self.b_cbf], writes=[self.b_ps[r]])
            S_.op("dve", CP(hT[:, chunk, tokblk * 128:(tokblk + 1) * 128], pb), reads=[self.b_ps[r]], writes=[b_hT[0]])

        def finish65(chunk):
            def fin(gl, ns, accap):
                slots = [nxt("mixtm", 4) for _ in range(4)]
                for a in range(2):
                    ri = nxt("rden", 4)
                    bank = accap(a, 0)[0]
                    den = self.ps[bank][:, 0:260].rearrange("p (n w) -> p n w", w=65)[:, :, 64:65]
                    S_.op("dve", RCP(self.rden[ri][:, 0:4].unsqueeze(2), den), reads=b_acc[bank], writes=[b_rden[ri]])
                    for nn in range(4):
                        _, sub, aap = accap(a, nn)
                        S_.op("dve", TS(self.mixtm[slots[nn]][:, a * 64:(a + 1) * 64], aap[:, 0:64],
                                        self.rden[ri][:, nn:nn + 1], None, ALU.mult),
                              reads=b_acc[bank] + [b_rden[ri]], writes=[b_mixtm[slots[nn]]])
                for nn in range(4):
                    put_mixT(slots[nn], chunk, gl * 4 + nn)
            return fin

        def finish_diff(chunk):
            def fin(gl, ns, accap):
                ri = nxt("rden", 4)
                ri2 = nxt("rden", 4)
                rd = self.rden[ri]
                rd2 = self.rden[ri2]
                for a in range(2):
                    for hb in range(2):
                        bank = 4 + 2 * a + hb
                        den = self.ps[bank][:, 0:258].rearrange("p (n w) -> p n w", w=129)[:, :, 128:129]
                        S_.op("dve", RCP(rd[:, 4 * a + 2 * hb:4 * a + 2 * hb + 2].unsqueeze(2), den),
                              reads=b_acc[bank], writes=[b_rden[ri]])
                S_.op("dve", TS(rd[:, 4:8], rd[:, 4:8], self.stat[:, 16:17], None, ALU.mult), reads=[self.b_stat], writes=[b_rden[ri]])
                ys = []
                sls = []
                for nn in range(4):
                    yi = nxt("ytmp", 4)
                    sl = nxt("mixtm", 4)
                    b1, s1, a1 = accap(0, nn)
                    b2, s2, a2 = accap(1, nn)
                    y = self.ytmp[yi // 2][:, yi % 2, :]
                    S_.op("act", ACT(y, a2[:, 0:128], AF.Copy, scale=rd[:, 4 + nn:5 + nn]),
                          reads=b_acc[b2] + [b_rden[ri]], writes=[b_ytmp[yi // 2]])
                    S_.op("dve", STT(y, a1[:, 0:128], rd[:, nn:nn + 1], y, ALU.mult, ALU.add),
                          reads=b_acc[b1] + [b_rden[ri]], writes=[b_ytmp[yi // 2]])
                    ys.append((y, yi))
                    sls.append(sl)
                for nn in range(4):
                    y, yi = ys[nn]
                    sl = sls[nn]
                    S_.op("dve", lambda e, y=y, sl=sl, nn=nn, rd2=rd2: e.scalar_tensor_tensor(
                        self.mixtm[sl][:], y, 1.0, y, ALU.mult, ALU.mult, accum_out=rd2[:, nn:nn + 1]),
                        reads=[b_ytmp[yi // 2]], writes=[b_mixtm[sl], b_rden[ri2]])
                S_.op("act", ACT(rd2[:, 4:8], rd2[:, 0:4], AF.Ln, bias=self.cst[:, C_EPS:C_EPS + 1], scale=1.0 / 128),
                      reads=[self.b_cst], writes=[b_rden[ri2]])
                S_.op("act", ACT(rd2[:, 4:8], rd2[:, 4:8], AF.Exp, scale=-0.5), writes=[b_rden[ri2]])
                for nn in range(4):
                    y, yi = ys[nn]
                    S_.op("dve", STT(self.mixtm[sls[nn]][:], y, rd2[:, 4 + nn:5 + nn], self.cst[:, C_SUBG:C_SUBG + 128], ALU.mult, ALU.mult),
                          reads=[b_ytmp[yi // 2], b_rden[ri2], self.b_cst], writes=[b_mixtm[sls[nn]]])
                    put_mixT(sls[nn], chunk, gl * 4 + nn)
            return fin

        def load_kv(layer, tile, hp):
            i = nxt("kv", 2)
            ntok = 1024 * (tile + 1)
            kg = self.kT0_g if layer == 0 else self.kT1_g
            vg = self.v0_g if layer == 0 else self.v1_g
            VWp = 130 if layer == 0 else 129
            Vv = self.aview(Vraw[i], BF16, [2, 16, VWp])
            for t in range(tile + 1):
                for r in range(2):
                    S_.dma("sp", kT_sb[i][:, r, t * 1024:(t + 1) * 1024], kg[t][r * 768 + hp * 128:r * 768 + (hp + 1) * 128, :],
                           b_kT[i], writes=[b_kT[i]])
                    S_.dma("sp", Vv[:, r, t * 8:(t + 1) * 8, :],
                           vg[t].rearrange("(r n p) w -> p r n w", r=2, p=128)[:, r, :, hp * VWp:(hp + 1) * VWp],
                           b_V[i], writes=[b_V[i]])
            return i, Vv

        def load_kv_into(layer, tile, hp, i):
            kg = self.kT0_g if layer == 0 else self.kT1_g
            vg = self.v0_g if layer == 0 else self.v1_g
            VWp = 130 if layer == 0 else 129
            Vv = self.aview(Vraw[i], BF16, [2, 16, VWp])
            for t in range(tile + 1):
                for r in range(2):
                    S_.dma("sp", kT_sb[i][:, r, t * 1024:(t + 1) * 1024], kg[t][r * 768 + hp * 128:r * 768 + (hp + 1) * 128, :],
                           b_kT[i], reads=b_gate[layer][t], writes=[b_kT[i]])
                    S_.dma("sp", Vv[:, r, t * 8:(t + 1) * 8, :],
                           vg[t].rearrange("(r n p) w -> p r n w", r=2, p=128)[:, r, :, hp * VWp:(hp + 1) * VWp],
                           b_V[i], reads=b_gate[layer][t], writes=[b_V[i]])

        def attention(layer, tile, DEPTH=2):
            nhp = 6
            VWp = 130 if layer == 0 else 129
            steps = []
            slots = [nxt("kv", 2) for _ in range(nhp)]
            load_kv_into(layer, tile, 0, slots[0])
            load_kv_into(layer, tile, 1, slots[1])
            for hp in range(nhp):
                i = slots[hp]
                Vv = self.aview(Vraw[i], BF16, [2, 16, VWp])
                units = []
                for a in range(2):
                    rows = slice(a * 64, (a + 1) * 64)

                    def kf(j, rows=rows, i=i):
                        r, n = gidx(j)
                        return kT_sb[i][rows, r, n * 128:(n + 1) * 128], b_kT[i]

                    def qf(t0, nt, rows=rows, hp=hp):
                        return qT[rows, hp, t0:t0 + nt], b_qT

                    if layer == 0:
                        def vf(j, a=a, i=i, Vv=Vv):
                            r, n = gidx(j)
                            return Vv[:, r, n, a * 65:(a + 1) * 65], b_V[i]
                        h = 2 * hp + a

                        def prep(gl, ns, a=a, h=h):
                            if a == 0:
                                self._bg = nxt("biasG", 2)
                            bg = self._bg
                            for nn, n in enumerate(ns):
                                Jn = JN(n)
                                cin = ct[:, CSEQ:CSEQ + 384].rearrange("p (j h) -> p j h", h=12)[:, 0:Jn, h]
                                S_.op("dve", TS(self.biasG[bg][:, a, nn, 0:Jn], cin, ct[:, CREF + n * 12 + h:CREF + n * 12 + h + 1],
                                                None, ALU.subtract), reads=[b_ctab], writes=[b_biasG[bg]])
                            return bg

                        def bf(bg, nn, j, a=a):
                            return self.biasG[bg][:, a, nn, j:j + 1], b_biasG[bg]
                        units.append(dict(q=qf, k=kf, v=vf, bias=bf, prep=prep, head=h))
                    else:
                        def vf(j, i=i, Vv=Vv):
                            r, n = gidx(j)
                            return Vv[:, r, n, :], b_V[i]
                        units.append(dict(q=qf, k=kf, v=vf))
                n0 = len(steps)
                if layer == 0:
                    attend(steps, tile, units, JN, True, 65, finish65(hp))
                else:
                    attend(steps, tile, units, JN, True, 129, finish_diff(hp))
                if hp + 2 < nhp:
                    f_, b_ = steps[-1]

                    def back2(b_=b_, hp=hp):
                        b_()
                        load_kv_into(layer, tile, hp + 2, slots[hp + 2])
                    steps[-1] = (f_, back2)
            for pm in range(2):
                units = []
                for a in range(2):
                    rows = slice(a * 64, (a + 1) * 64)

                    def kf(j, rows=rows, pm=pm):
                        return self.memKT[rows, pm, j * 128:(j + 1) * 128], b_memK

                    def qf(t0, nt, rows=rows, pm=pm):
                        return mqT[rows, pm, t0:t0 + nt], b_mqT

                    def vf(j, a=a, pm=pm):
                        return self.memV[:, j, 2 * pm + a, :], b_memV
                    units.append(dict(q=qf, k=kf, v=vf))
                attend(steps, tile, units, lambda n: 2, False, 65, finish65(6 + pm))
            assert len(steps) % 2 == 0
            npair = len(steps) // 2
            for p in range(npair + 2):
                cur_pair[0] = p
                run_deferred()
                if 0 <= 2 * p - 3 < len(steps):
                    steps[2 * p - 3][1]()
                if p < npair:
                    steps[2 * p][0]()
                    steps[2 * p + 1][0]()
                if 0 <= 2 * p - 2 < len(steps):
                    steps[2 * p - 2][1]()
            run_deferred(flush=True)
            cur_pair[0] = 0

        def q_proj(layer, tile):
            w = self.w_a_in if layer == 0 else self.w_b_in
            if layer == 1:
                cosT = self.aview(A_COS, F32, [1024])
                sinT = self.aview(A_COS + 4096, F32, [1024])
                b_cs = b_kT[0]
                S_.dma("sp", cosT, self.cos_in[:, tile * 1024:(tile + 1) * 1024], b_cs, writes=[b_cs])
                S_.dma("sp", sinT, self.sin_in[:, tile * 1024:(tile + 1) * 1024], b_cs, writes=[b_cs])
            for oc in range(8):
                dst, dbuf = (qT, b_qT) if oc < 6 else (mqT, b_mqT)
                dc = oc if oc < 6 else oc - 6
                col0 = (oc * 128 if oc < 6 else (2316 + dc * 128)) if layer == 0 else oc * 128
                if layer == 1 and oc < 6:
                    self.rope_proj(w, col0, hT, b_hT, cosT, sinT, b_cs, ws_next, wview,
                                   lambda tt, dc=dc: (qT[:, dc, tt * 512:(tt + 1) * 512], b_qT))
                else:
                    wt, wb = ws128_next()
                    self.wload(wt[:, :, 0:128], wview(w, 0, col0, 128), wb)

                    def evac(tt, ps, pbuf, dst=dst, dbuf=dbuf, dc=dc):
                        S_.op("act", ACT(dst[:, dc, tt * 512:(tt + 1) * 512], ps[:, :], AF.Copy), reads=[pbuf], writes=[dbuf])
                    self.proj_fm(wt, wb, 0, hT, b_hT, 1024, evac)

        def out_proj(layer, tile):
            for nb in range(8):
                n = tile * 8 + nb
                for half in range(2):
                    r = self.ring()
                    S_.op("pe", [MM(self.ps[r][:, :], hT[:, c, nb * 128:(nb + 1) * 128], wo[:, c, half * 512:(half + 1) * 512],
                                    start=(c == 0), stop=(c == 7)) for c in range(8)],
                          reads=[b_hT[0], b_wo], writes=[self.b_ps[r]])
                    xs = self.x_sb[:, n, half * 512:(half + 1) * 512]
                    S_.op("dve", TT(xs, xs, self.ps[r][:, :], ALU.add), reads=[self.b_ps[r]], writes=[self.b_x[n]])
                self.x_sq(n)
            self.x_rstd(tile)

        def ffn(layer, tile):
            norm_tile(tile, C_G_FFN0 if layer == 0 else C_G_FFN1, hT, b_hT[0])
            fence(alias_Z)
            for f in range(NFC):
                wgt, wgb = ws_next()
                wgt = wgt.rearrange("p c (g f) -> p c g f", g=2)
                self.wload(wgt[:, :, 0, :], wview(self.w_gu, layer * D, f * 128, 128), wgb)
                self.wload(wgt[:, :, 1, :], wview(self.w_gu, layer * D, DFF + f * 128, 128), wgb)
                for tt in range(2):
                    self._fr = (getattr(self, "_fr", -1) + 1) % 8
                    rg = self._fr
                    self._fr = (self._fr + 1) % 8
                    ru = self._fr
                    S_.op("pe", [MM(self.ps[rg][:, :], wgt[:, c, 0, :], hT[:, c, tt * 512:(tt + 1) * 512],
                                    start=(c == 0), stop=(c == 7)) for c in range(8)],
                          reads=[wgb, b_hT[0]], writes=[self.b_ps[rg]])
                    S_.op("pe", [MM(self.ps[ru][:, :], wgt[:, c, 1, :], hT[:, c, tt * 512:(tt + 1) * 512],
                                    start=(c == 0), stop=(c == 7)) for c in range(8)],
                          reads=[wgb, b_hT[0]], writes=[self.b_ps[ru]])
                    s = nxt("pT", 4)
                    S_.op("act", ACT(self.pT[s][:], self.ps[rg][:, :], AF.Silu), reads=[self.b_ps[rg]], writes=b_pT[s])
                    S_.op("dve", TT(actT[:, f, tt * 512:(tt + 1) * 512], self.pT[s][:], self.ps[ru][:, :], ALU.mult),
                          reads=b_pT[s] + [self.b_ps[ru]], writes=[b_act])
            for tt in range(2):
                for f in range(NFC):
                    wi = nxt("wdn", NWDN)
                    self.wload(wdn[wi], self.w_dn[layer * DFF + f * 128:layer * DFF + (f + 1) * 128, :], b_wdn[wi])
                    fns = []
                    for nb in range(4):
                        for half in range(2):
                            fns.append(MM(self.ps[nb * 2 + half][:, :], actT[:, f, tt * 512 + nb * 128:tt * 512 + (nb + 1) * 128],
                                          wdn[wi][:, half * 512:(half + 1) * 512], start=(f == 0), stop=(f == NFC - 1)))
                    if f == 0:
                        for bi, fn in enumerate(fns):
                            S_.op("pe", fn, reads=[b_act, b_wdn[wi]], writes=[self.b_ps[bi]])
                    else:
                        S_.op("pe", fns, reads=[b_act, b_wdn[wi]], writes=self.b_ps)
                for nb in range(4):
                    n = tile * 8 + tt * 4 + nb
                    for half in range(2):
                        xs = self.x_sb[:, n, half * 512:(half + 1) * 512]
                        S_.op("dve", TT(xs, xs, self.ps[nb * 2 + half][:, :], ALU.add),
                              reads=[self.b_ps[nb * 2 + half]], writes=[self.b_x[n]])
                    self.x_sq(n)
            self.x_rstd(tile)
            fence(alias_Z)

        def final_norm(tile):
            for nb in range(8):
                n = tile * 8 + nb
                i = self._rms_i = (getattr(self, "_rms_i", 0) + 1) % 2
                ss = self.stat[:, 2 * i:2 * i + 1]
                rstd = self.stat[:, 2 * i + 1:2 * i + 2]
                bst = self.b_rst[i]
                xs = self.x_sb[:, n, :]
                rstd = self.statx[:, 16 + n:17 + n]
                S_.op("dve", STT(xs, xs, rstd, self.cst[:, C_FING:C_FING + 1024], ALU.mult, ALU.mult),
                      reads=[self.b_statx, self.b_cst], writes=[self.b_x[n]])
                S_.dma("sp", self.out_d[n * 128:(n + 1) * 128, :], xs, self.b_x[n], reads=[self.b_x[n]])

        def lam_setup():
            t = self.ytmp[0]
            for k in range(2):
                S_.op("dve", TT(t[:, k, 0:64], self.cst[:, C_LAM + 128 * k:C_LAM + 128 * k + 64],
                                self.cst[:, C_LAM + 128 * k + 64:C_LAM + 128 * k + 128], ALU.mult),
                      reads=[self.b_cst], writes=[b_ytmp[0]])
                S_.op("dve", lambda e, k=k: e.tensor_reduce(self.stat[:, 12 + k:13 + k], t[:, k, 0:64],
                                                             mybir.AxisListType.X, ALU.add),
                      reads=[b_ytmp[0]], writes=[self.b_stat])
                S_.op("act", ACT(self.stat[:, 14 + k:15 + k], self.stat[:, 12 + k:13 + k], AF.Exp), writes=[self.b_stat])
            S_.op("dve", TS(self.cst[:, C_SUBG:C_SUBG + 128], self.cst[:, C_SUBG:C_SUBG + 128], 1.0 - LAM_INIT, None, ALU.mult),
                  writes=[self.b_cst])
            S_.op("dve", TS(self.stat[:, 16:17], self.stat[:, 15:16], self.stat[:, 14:15], -LAM_INIT, ALU.subtract, ALU.add),
                  writes=[self.b_stat])

        def exchange(layer, tile):
            kb, kg = (self.kT0_b, self.kT0_g) if layer == 0 else (self.kT1_b, self.kT1_g)
            vb, vg = (self.v0_b, self.v0_g) if layer == 0 else (self.v1_b, self.v1_g)
            items = [(f"kT{layer}{tile}", kb[tile], kg[tile], b_kstg), (f"v{layer}{tile}", vb[tile], vg[tile], b_vstg)]
            if layer == 0 and tile == 1:
                items += [("lf0", self.lf_b, self.lf_g, [b_lfstg])]
            for name, own, full, bufs in items:
                gt = Buf("gate_" + name)
                S_.collective(own, full, groups, name, reads=[], writes=[gt], after=list(bufs))
                if name == "lf0":
                    b_gate_lf.append(gt)
                else:
                    b_gate[layer][tile].append(gt)

        def dump_x(dst):
            for n in range(NB):
                S_.dma("sp", dst[n * 128:(n + 1) * 128, :], self.x_sb[:, n, :], self.b_x[n], reads=[self.b_x[n]])

        def load_wo(layer):
            self.wload(wo, wview(self.w_out, layer * D, 0, 1024), b_wo)

        dbg = self.dbg
        if P2:
            mem_kv(0)
            self.load_x(self.x_in)
            for t_ in range(2):
                for nb_ in range(8):
                    self.x_sq(t_ * 8 + nb_)
                self.x_rstd(t_)
        if P1:
            for tile in range(2):
                kv_pass(0, tile)
                if fused:
                    exchange(0, tile)
        if P2:
            fence(alias_Z)
            norm_tile(0, C_G_ATT0, hT, b_hT[0])
            q_proj(0, 0)
            ctables()
            if dbg == "ct":
                S_.dma("sp", self.dbg_s[:, 0:1612], ct[:, 0:1612], b_ctab, reads=[b_ctab])
                return
            if dbg == "mem":
                S_.dma("pool", self.dbg_s[:, 0:512], self.memKT[:].rearrange("p a b -> p (a b)"), b_memK, reads=[b_memK])
                S_.dma("pool", self.dbg_s[:, 512:512 + 520], self.memV[:].rearrange("p a b c -> p (a b c)"), b_memV, reads=[b_memV])
                return
            for tile in range(2):
                if tile > 0:
                    fence(alias_Z)
                    norm_tile(tile, C_G_ATT0, hT, b_hT[0])
                    q_proj(0, tile)
                if tile == 0:
                    load_wo(0)
                if dbg == "qp":
                    S_.dma("pool", self.dbg_s[:, 0:6144], qT.rearrange("p a b -> p (a b)"), b_qT, reads=[b_qT])
                    S_.dma("pool", self.dbg_s[:, 6144:8192], mqT.rearrange("p a b -> p (a b)"), b_mqT, reads=[b_mqT])
                    return
                attention(0, tile)
                if dbg == "attn0":
                    break
                out_proj(0, tile)
                if dbg == "op0":
                    dump_x(self.dbg_d)
                    return
                ffn(0, tile)
                if dbg == "ffn0":
                    dump_x(self.dbg_d)
                    return
                kv_pass(1, tile)
                if fused:
                    exchange(1, tile)
                if dbg == "kv10":
                    dump_x(self.dbg_d)
                    return
            if dbg == "attn0":
                S_.dma("pool", self.dbg_m.rearrange("p (c f) -> p c f", c=8), hT, b_hT[0], reads=[b_hT[0]])
                return
            if not fused:
                dump_x(self.xmid_d)
        if dbg == "L0":
            dump_x(self.dbg_d)
            return
        if P3:
            lam_setup()
            for tile in range(2):
                fence(alias_Z)
                norm_tile(tile, C_G_ATT1, hT, b_hT[0])
                q_proj(1, tile)
                if tile == 0:
                    mem_kv(1)
                    load_wo(1)
                attention(1, tile)
                out_proj(1, tile)
                ffn(1, tile)
                final_norm(tile)

    def rope_proj(self, w, c0, hT, b_hT, cosT, sinT, b_cs, ws_next, wview, dstf, tmp=None):
        S_ = self.S
        wt, wb = ws_next()
        self.wload(wt[:, :, 0:128], wview(w, 0, c0, 128), wb)
        src_v = wt[:, :, 0:128].rearrange("p c (b h f) -> p c b h f", b=2, h=2)
        dst_v = wt[:, :, 128:256].rearrange("p c (b h f) -> p c b h f", b=2, h=2)
        for hf in range(2):
            S_.op("act", ACT(dst_v[:, :, :, hf, :], src_v[:, :, :, 1 - hf, :], AF.Copy), reads=[wb], writes=[wb])
        for tt in range(2):
            ra = self.ring()
            rb = self.ring()
            S_.op("pe", [MM(self.ps[ra][:, :], wt[:, c, 0:128], hT[:, c, tt * 512:(tt + 1) * 512], start=(c == 0), stop=(c == 7))
                         for c in range(8)], reads=[wb, b_hT[0]], writes=[self.b_ps[ra]])
            S_.op("pe", [MM(self.ps[rb][:, :], wt[:, c, 128:256], hT[:, c, tt * 512:(tt + 1) * 512], start=(c == 0), stop=(c == 7))
                         for c in range(8)], reads=[wb, b_hT[0]], writes=[self.b_ps[rb]])
            i = self._rt_i = (getattr(self, "_rt_i", 0) + 1) % 2
            ta = self.ropet[i][:, 0, :]
            tb2 = self.ropet[i][:, 1, :]
            S_.op("dve", TT(ta, self.ps[ra][:, :], cosT[:, tt * 512:(tt + 1) * 512], ALU.mult),
                  reads=[self.b_ps[ra], b_cs], writes=[self.b_ropet[i]])
            S_.op("dve", TT(tb2, self.ps[rb][:, :], sinT[:, tt * 512:(tt + 1) * 512], ALU.mult),
                  reads=[self.b_ps[rb], b_cs], writes=[self.b_ropet[i]])
            dap, dbuf = dstf(tt)
            S_.op("dve", TT(dap, ta, tb2, ALU.add), reads=[self.b_ropet[i]], writes=[dbuf])


def build_nc(mode="ALL", dbg=None):
    return Builder(mode, dbg).build()


def _bf(a):
    return np.ascontiguousarray(a).astype(ml_dtypes.bfloat16)


def own_rows(p):
    return np.concatenate([np.arange(seq_block(p, n) * 128, seq_block(p, n) * 128 + 128) for n in range(NB)])


def host_consts(inputs, p):
    f = lambda a: np.asarray(a, np.float32)
    cst = np.zeros((128, NCST), np.float32)

    def gT(v):
        return f(v).reshape(8, 128).T

    cst[:, C_G_ATT0:C_G_ATT0 + 8] = gT(inputs["attn_norm_g"][0])
    cst[:, C_G_ATT1:C_G_ATT1 + 8] = gT(inputs["attn_norm_g"][1])
    cst[:, C_G_FFN0:C_G_FFN0 + 8] = gT(inputs["ffn_norm_g"][0])
    cst[:, C_G_FFN1:C_G_FFN1 + 8] = gT(inputs["ffn_norm_g"][1])
    cst[:, C_G_KV:C_G_KV + 8] = gT(inputs["kv_norm_g"])
    cst[:, C_G_MEM0:C_G_MEM0 + 8] = gT(inputs["mem_norm_g"][0])
    cst[:, C_G_MEM1:C_G_MEM1 + 8] = gT(inputs["mem_norm_g"][1])
    cst[:, C_BF:C_BF + 12] = f(inputs["a_b_f"][0])[None, :]
    cst[:, C_PAR] = float(p)
    for i, k in enumerate(("b_lambda_q1", "b_lambda_k1", "b_lambda_q2", "b_lambda_k2")):
        cst[:, C_LAM + 64 * i:C_LAM + 64 * (i + 1)] = f(inputs[k][0])[None, :]
    cst[:, C_SUBG:C_SUBG + 128] = f(inputs["b_subln_g"][0])[None, :]
    cst[:, C_FING:C_FING + 1024] = f(inputs["final_norm_g"])[None, :]
    cst[:, C_ONE] = 1.0
    cst[:, C_EPS] = EPS
    k = np.arange(128)[:, None]
    q = np.arange(128)[None, :]
    tri = (q >= k).astype(np.float32)
    ones = np.ones((128, 128), np.float32)
    zeros = np.zeros((128, 128), np.float32)
    if p == 0:
        masks = [tri, zeros, ones, tri]
    else:
        masks = [ones, tri, tri, zeros]
    cbf = _bf(np.concatenate([np.eye(128, dtype=np.float32)] + masks, axis=1))
    cf32 = np.concatenate([(k <= q).astype(np.float32), ones], axis=1)
    pos = own_rows(p).astype(np.float64)
    inv = np.power(10000.0, -np.arange(32, dtype=np.float64) * (2.0 / 64))
    ang = pos[None, :] * inv[np.arange(128) % 32][:, None]
    cosT = np.cos(ang).astype(np.float32)
    sgn = np.where((np.arange(128) % 64) < 32, -1.0, 1.0)[:, None]
    sinT = (np.sin(ang) * sgn).astype(np.float32)
    return cst, cbf, cf32, cosT, sinT


def core_inputs(inputs, c, x_src=None):
    b, p = divmod(c, 2)
    f = lambda a: np.ascontiguousarray(np.asarray(a, np.float32))
    cst, cbf, cf32, cosT, sinT = host_consts(inputs, p)
    xs = f(inputs["x"][b]) if x_src is None else x_src[b]
    return {
        "x_own": np.ascontiguousarray(xs[own_rows(p)]),
        "mem_b": f(inputs["mem"][b]),
        "cst": cst, "cbf": cbf, "cf32": cf32, "cosT": cosT, "sinT": sinT,
        "a_w_in": f(inputs["a_w_in"][0]),
        "b_w_in": f(inputs["b_w_in"][0]),
        "w_out": f(inputs["w_out"]).reshape(2 * D, D),
        "w_gate_up": f(inputs["w_gate_up"]).reshape(2 * D, 2 * DFF),
        "w_down": f(inputs["w_down"]).reshape(2 * DFF, D),
        "w_mem_kv": f(inputs["w_mem_kv"]).reshape(2 * D, 512),
        "w_kv_shared": f(inputs["w_kv_shared"]),
    }


def kernel(**inputs):
    nc = build_nc("ALL")
    maps = [core_inputs(inputs, c) for c in range(8)]
    res = run_bass_kernel_spmd(nc, maps, core_ids=list(range(8)))
    out = np.zeros((4, S, D), np.float32)
    for c in range(8):
        b, p = divmod(c, 2)
        out[b][own_rows(p)] = np.asarray(res.results[c]["out_own"], np.float32)
    return out
```

```python
import math
from contextlib import ExitStack

import ml_dtypes
import numpy as np

import concourse.bass as bass
import concourse.mybir as mybir
from concourse.bass_utils import run_bass_kernel_spmd

F32 = mybir.dt.float32
BF16 = mybir.dt.bfloat16
AF = mybir.ActivationFunctionType
ALU = mybir.AluOpType

D = 1024
S = 4096
NB = 16
T = NB * 128
DFF = 2816
NFC = DFF // 128
EPS = 1e-6
LAM_INIT = 0.8 - 0.6 * math.exp(-0.3 * 1)

C_G_ATT0, C_G_ATT1, C_G_FFN0, C_G_FFN1, C_G_KV, C_G_MEM0, C_G_MEM1 = [8 * i for i in range(7)]
C_BF = 56
C_PAR = 68
C_LAM = 69
C_SUBG = C_LAM + 256
C_FING = C_SUBG + 128
C_ONE = C_FING + 1024
C_EPS = C_ONE + 1
NCST = C_EPS + 3


def seq_block(p, n):
    m, e = divmod(n, 2)
    return 4 * m + ((0, 3)[e] if p == 0 else (1, 2)[e])


def JN(n):
    m, e = divmod(n, 2)
    return 4 * m + (2 if e == 0 else 4)


def gidx(j):
    m, q = divmod(j, 4)
    r = 0 if q in (0, 3) else 1
    e = 0 if q in (0, 1) else 1
    return r, 2 * m + e


def MM(out, lhsT, rhs, start=True, stop=True, skip=False):
    if skip:
        return lambda e: e.matmul(out, lhsT, rhs, start=start, stop=stop, skip_group_check=True)
    return lambda e: e.matmul(out, lhsT, rhs, start=start, stop=stop)


def TR(out, in_, ident):
    return lambda e: e.transpose(out, in_, ident)


def ACT(out, in_, func, bias=None, scale=None, accum=None):
    def f(e):
        kw = {}
        if bias is not None:
            kw["bias"] = bias
        if scale is not None:
            kw["scale"] = scale
        if accum is not None:
            kw["accum_out"] = accum
        return e.activation(out, in_, func, **kw)
    return f


def TT(out, a, b, op):
    return lambda e: e.tensor_tensor(out, a, b, op)


def TS(out, a, s1, s2, op0, op1=None):
    if op1 is None:
        return lambda e: e.tensor_scalar(out, a, s1, None, op0)
    return lambda e: e.tensor_scalar(out, a, s1, s2, op0, op1)


def STT(out, in0, scalar, in1, op0, op1):
    return lambda e: e.scalar_tensor_tensor(out, in0, scalar, in1, op0, op1)


def CP(out, in_):
    return lambda e: e.tensor_copy(out, in_)


def MS(ap, val):
    return lambda e: e.memset(ap, val)


def RCP(out, in_):
    return lambda e: e.reciprocal(out, in_)


class Buf:
    __slots__ = ("name", "w", "r", "dsem", "dcnt")

    def __init__(self, name):
        self.name = name
        self.w = {}
        self.r = {}
        self.dsem = None
        self.dcnt = 0


class Sched:
    CE = ("pe", "act", "dve", "pool")
    ALL = ("pe", "act", "dve", "pool", "sp")

    def __init__(self, nc, stack):
        self.nc = nc
        self.stack = stack
        self.prog = {e: [] for e in self.ALL}
        self.cnt = {e: 0 for e in self.CE}
        self.csem = {e: self.newsem("c_" + e) for e in self.CE}
        self.waited = {e: {} for e in self.ALL}
        self.semobj = {}
        self.dbufs = []
        self.nsem = 4

    def newsem(self, name):
        return self.stack.enter_context(self.nc.semaphore(name))

    def _wait(self, e, key, sem, val):
        if self.waited[e].get(key, 0) >= val:
            return
        self.waited[e][key] = val
        self.prog[e].append(lambda eng: eng.wait_ge(sem, val))

    def _deps(self, e, reads, writes):
        deps = {}
        for b in reads:
            for k, v in b.w.items():
                if deps.get(k, (None, 0))[1] < v[1]:
                    deps[k] = v
        for b in writes:
            for dd in (b.w, b.r):
                for k, v in dd.items():
                    if deps.get(k, (None, 0))[1] < v[1]:
                        deps[k] = v
        for k, (sem, val) in deps.items():
            self._wait(e, k, sem, val)

    def _record(self, key, sem, val, reads, writes):
        for b in reads:
            if b.r.get(key, (None, 0))[1] < val:
                b.r[key] = (sem, val)
        for b in writes:
            b.w = {key: (sem, val)}
            b.r = {}

    def op(self, e, fns, reads=(), writes=()):
        if not isinstance(fns, (list, tuple)):
            fns = [fns]
        self._deps(e, reads, writes)
        self.cnt[e] += 1
        sem = self.csem[e]
        for f in fns[:-1]:
            self.prog[e].append(f)
        last = fns[-1]
        self.prog[e].append(lambda eng: last(eng).then_inc(sem, 1))
        self._record("c_" + e, sem, self.cnt[e], reads, writes)

    def dma(self, q, out, in_, sbuf, reads=(), writes=()):
        own = "d_" + sbuf.name
        saved = None
        if sbuf in writes and sbuf not in reads and own in sbuf.w:
            saved = sbuf.w.pop(own)
        self._deps(q, reads, writes)
        if saved is not None:
            sbuf.w[own] = saved
        if sbuf.dsem is None:
            sbuf.dsem = self.newsem("d_" + sbuf.name)
            self.nsem += 1
            self.dbufs.append(sbuf)
        sbuf.dcnt += 16
        sem = sbuf.dsem
        self.prog[q].append(lambda eng: eng.dma_start(out=out, in_=in_).then_inc(sem, 16))
        self._record("d_" + sbuf.name, sem, sbuf.dcnt, reads, writes)

    def collective(self, ins, outs, groups, name, reads, writes, after=()):
        self._deps("pool", reads, list(writes) + list(after))
        sem = self.newsem("cc_" + name)
        self.nsem += 1
        self.prog["pool"].append(
            lambda eng: eng.collective_compute("AllGather", ALU.bypass, replica_groups=groups,
                                               ins=[ins.opt()], outs=[outs.opt()]).then_inc(sem))
        self._record("cc_" + name, sem, 1, reads, writes)

    def barrier(self):
        for e in self.ALL:
            for c in self.CE:
                if self.cnt[c]:
                    self._wait(e, "c_" + c, self.csem[c], self.cnt[c])
            for b in self.dbufs:
                self._wait(e, "d_" + b.name, b.dsem, b.dcnt)

    def finish(self):
        for b in self.dbufs:
            self._wait("sp", "d_" + b.name, b.dsem, b.dcnt)

    def emit(self, block):
        prog = self.prog

        @block.tensor
        def _(eng):
            for f in prog["pe"]:
                f(eng)

        @block.scalar
        def _(eng):
            for f in prog["act"]:
                f(eng)

        @block.vector
        def _(eng):
            for f in prog["dve"]:
                f(eng)

        @block.gpsimd
        def _(eng):
            for f in prog["pool"]:
                f(eng)

        @block.sync
        def _(eng):
            for f in prog["sp"]:
                f(eng)


class Builder:
    def __init__(self, mode="ALL", dbg=None):
        self.mode = mode
        self.dbg = dbg
        self.nc = bass.Bass("TRN2", target_bir_lowering=False)
        self.io = {}

    def din(self, name, shape, dt=F32):
        t = self.nc.dram_tensor(name, list(shape), dt, kind="ExternalInput").ap()
        self.io[name] = t
        return t

    def dout(self, name, shape, dt=F32):
        t = self.nc.dram_tensor(name, list(shape), dt, kind="ExternalOutput").ap()
        self.io[name] = t
        return t

    def dscr(self, name, shape, dt):
        return self.nc.dram_tensor(name, list(shape), dt).ap()

    def exch(self, name, shape_own, dt, own_kind, full_kind):
        mk = {"scr": self.dscr, "in": self.din, "out": self.dout}
        own = mk[own_kind](name + "_b", shape_own, dt) if own_kind else None
        full = mk[full_kind](name + "_g", [2 * shape_own[0], shape_own[1]], dt) if full_kind else None
        return own, full

    def aview(self, off, dt, shape):
        esz = 4 if dt == F32 else 2
        n = int(np.prod(shape)) * esz
        assert off % 4 == 0 and n % 4 == 0
        assert off + n <= self.ARENA, (off, n, self.ARENA)
        ap = self.arena[:, off // 4:(off + n) // 4]
        if dt != F32:
            ap = ap.bitcast(dt)
        if len(shape) == 2:
            ap = ap.rearrange("p (a b) -> p a b", a=shape[0])
        elif len(shape) == 3:
            ap = ap.rearrange("p (a b c) -> p a b c", a=shape[0], b=shape[1])
        return ap

    def build(self):
        nc = self.nc
        mode = self.mode
        fused = mode == "ALL"
        P1 = mode in ("ALL", "P1")
        P2 = mode in ("ALL", "P2")
        P3 = mode in ("ALL", "P3")

        self.x_in = self.din("x_own", [T, D])
        self.mem_in = self.din("mem_b", [256, D])
        self.cst_in = self.din("cst", [128, NCST])
        self.cbf_in = self.din("cbf", [128, 5 * 128], BF16)
        self.cf_in = self.din("cf32", [128, 256])
        self.cos_in = self.din("cosT", [128, T])
        self.sin_in = self.din("sinT", [128, T])
        self.w_a_in = self.din("a_w_in", [D, 2572])
        self.w_b_in = self.din("b_w_in", [D, D])
        self.w_out = self.din("w_out", [2 * D, D])
        self.w_gu = self.din("w_gate_up", [2 * D, 2 * DFF])
        self.w_dn = self.din("w_down", [2 * DFF, D])
        self.w_mem = self.din("w_mem_kv", [2 * D, 512])
        self.w_kvs = self.din("w_kv_shared", [D, 1536])

        def kinds(prod_phase, cons_phase):
            if fused:
                return "scr", "scr"
            return ("out" if mode == prod_phase else None), ("in" if mode == cons_phase else None)

        ko, kf = kinds("P1", "P2")
        self.kT0_b, self.kT0_g = zip(*[self.exch(f"kT0t{t}", [768, 1024], BF16, ko, kf) for t in range(2)])
        self.v0_b, self.v0_g = zip(*[self.exch(f"v0t{t}", [1024, 780], BF16, ko, kf) for t in range(2)])
        self.lf_b, self.lf_g = self.exch("lf0", [T, 12], F32, ko, kf)
        ko, kf = kinds("P2", "P3")
        self.kT1_b, self.kT1_g = zip(*[self.exch(f"kT1t{t}", [768, 1024], BF16, ko, kf) for t in range(2)])
        self.v1_b, self.v1_g = zip(*[self.exch(f"v1t{t}", [1024, 774], BF16, ko, kf) for t in range(2)])
        if fused or mode == "P3":
            self.out_d = self.dout("out_own", [T, D])
        if mode in ("P1", "P2"):
            self.xmid_d = self.dout("x_mid", [T, D]) if mode == "P2" else None
        self.dbg_out = {}
        if self.dbg in ('L0', 'op0', 'ffn0', 'kv10'):
            self.dbg_d = self.dout('dbg', [T, D])
        if self.dbg in ('ct', 'mem', 'qp'):
            self.dbg_s = self.dout('dbg_s', [128, 8192])
        if self.dbg == 'attn0':
            self.dbg_m = self.dout('dbg_m', [128, 8 * 1024])

        self.ARENA = 111 * 1024
        with ExitStack() as st:
            self.st = st
            E = st.enter_context
            self.x_sb = E(nc.sbuf_tensor("x_sb", [128, NB, D], F32))
            self.arena = E(nc.sbuf_tensor("arena", [128, self.ARENA // 4], F32))
            self.cst = E(nc.sbuf_tensor("cst_sb", [128, NCST], F32))
            self.cbf = E(nc.sbuf_tensor("cbf_sb", [128, 5 * 128], BF16))
            self.cf = E(nc.sbuf_tensor("cf_sb", [128, 256], F32))
            self.htm = [E(nc.sbuf_tensor(f"htm{i}", [128, D], BF16)) for i in range(2)]
            self.pT = [E(nc.sbuf_tensor(f"pT{i}", [128, 512], BF16)) for i in range(4)]
            self.stat = E(nc.sbuf_tensor("stat", [128, 64], F32))
            self.statx = E(nc.sbuf_tensor("statx", [128, 32], F32))
            self.b_statx = Buf("statx")
            self.ctab = E(nc.sbuf_tensor("ctab", [128, 32 * 12 + 16 * 12], F32))
            self.wdn_x = [E(nc.sbuf_tensor(f"wdnx{i}", [128, 1024], BF16)) for i in range(2)]
            self.biasG = [E(nc.sbuf_tensor(f"biasG{i}", [128, 2, 4, 32], F32)) for i in range(2)]
            self.rden = [E(nc.sbuf_tensor(f"rden{i}", [128, 8], F32)) for i in range(4)]
            self.mixtm = [E(nc.sbuf_tensor(f"mixtm{i}", [128, 128], BF16)) for i in range(4)]
            self.ytmp = [E(nc.sbuf_tensor(f"ytmp{i}", [128, 2, 128], F32)) for i in range(2)]
            self.memKT = E(nc.sbuf_tensor("memKT", [128, 2, 256], BF16))
            self.memV = E(nc.sbuf_tensor("memV", [128, 2, 4, 65], BF16))
            self.ps = [E(nc.psum_tensor(f"ps{i}", [128, 512], F32)) for i in range(8)]
            self.S = Sched(nc, st)
            self.b_ps = [Buf(f"ps{i}") for i in range(8)]
            self.ring_i = 0
            self.b_x = [Buf(f"x{n}") for n in range(NB)]
            self.b_cst = Buf("cst")
            self.b_htm = [Buf("htm0"), Buf("htm1")]
            self.b_stat = Buf("stat")
            self.ident = self.cbf[:, 0:128]

            block = E(nc.Block())
            self.program(P1, P2, P3, fused)
            self.S.finish()
            self.S.emit(block)
        return nc

    def ring(self):
        i = self.ring_i
        self.ring_i = (i + 1) % 4
        return i

    def load_consts(self):
        S_ = self.S
        S_.dma("sp", self.cst[:], self.cst_in, self.b_cst, writes=[self.b_cst])
        self.b_cbf = Buf("cbf")
        S_.dma("sp", self.cbf[:], self.cbf_in, self.b_cbf, writes=[self.b_cbf])
        self.b_cf = Buf("cf")
        S_.dma("sp", self.cf[:], self.cf_in, self.b_cf, writes=[self.b_cf])

    def load_x(self, src, first=None, rest=False):
        rng = range(NB) if first is None else (range(first, NB) if rest else range(first))
        for n in rng:
            self.S.dma("sp", self.x_sb[:, n, :], src[n * 128:(n + 1) * 128, :], self.b_x[n], writes=[self.b_x[n]])

    def wload(self, dst_ap, src_ap, buf):
        self.S.dma("pool", dst_ap, src_ap, buf, writes=[buf])

    def x_sq(self, n):
        i = self._rms_i = (getattr(self, "_rms_i", 0) + 1) % 2
        self.S.op("act", ACT(self.htm[i][:], self.x_sb[:, n, :], AF.Square, accum=self.statx[:, n:n + 1]),
                  reads=[self.b_x[n]], writes=[self.b_htm[i], self.b_statx])

    def x_rstd(self, tile):
        st = self.statx
        self.S.op("act", ACT(st[:, 16 + tile * 8:24 + tile * 8], st[:, tile * 8:tile * 8 + 8], AF.Ln,
                             bias=self.cst[:, C_EPS:C_EPS + 1], scale=1.0 / D), reads=[self.b_cst], writes=[self.b_statx])
        self.S.op("act", ACT(st[:, 16 + tile * 8:24 + tile * 8], st[:, 16 + tile * 8:24 + tile * 8], AF.Exp, scale=-0.5),
                  writes=[self.b_statx])

    def rms_T_multi(self, srcs, gcol, dstT, dst_buf, rstd=None):
        S_ = self.S
        nb = len(srcs)
        st = self.stat
        bst = self.b_rst[0]
        if rstd is None:
            for k, (ap, buf) in enumerate(srcs):
                junk = dstT[:, :, k * 128:(k + 1) * 128]
                S_.op("act", ACT(junk, ap.rearrange("p (c f) -> p c f", c=8), AF.Square, accum=st[:, 32 + k:33 + k]),
                      reads=[buf], writes=[dst_buf, bst])
            S_.op("act", ACT(st[:, 40:40 + nb], st[:, 32:32 + nb], AF.Sqrt, bias=self.cst[:, C_EPS:C_EPS + 1], scale=1.0 / D),
                  reads=[self.b_cst], writes=[bst])
            S_.op("dve", RCP(st[:, 40:40 + nb], st[:, 40:40 + nb]), writes=[bst])
            rstd = [st[:, 40 + k:41 + k] for k in range(nb)]
        else:
            bst = self.b_statx
        gT = self.cst[:, gcol:gcol + 8].unsqueeze(2).broadcast_to([128, 8, 128])
        for k, (ap, buf) in enumerate(srcs):
            i = self._rms_i = (getattr(self, "_rms_i", 0) + 1) % 2
            S_.op("act", ACT(self.htm[i][:], ap, AF.Copy, scale=rstd[k]), reads=[buf, bst],
                  writes=[self.b_htm[i]])
            r = self.ring()
            pb = self.ps[r][:, 0:512].bitcast(BF16).rearrange("p (a b) -> p a b", a=8)
            S_.op("pe", [TR(pb[:, c, :], self.htm[i][:, c * 128:(c + 1) * 128], self.ident) for c in range(8)],
                  reads=[self.b_htm[i], self.b_cbf], writes=[self.b_ps[r]])
            S_.op("dve", TT(dstT[:, :, k * 128:(k + 1) * 128], pb, gT, ALU.mult),
                  reads=[self.b_ps[r], self.b_cst], writes=[dst_buf])

    def proj_fm(self, wT, wbuf, c0, hT, hbufs, ntok, evac):
        for tt in range(ntok // 512):
            r = self.ring()
            self.S.op("pe", [MM(self.ps[r][:, :], wT[:, c, c0:c0 + 128], hT[:, c, tt * 512:(tt + 1) * 512],
                                start=(c == 0), stop=(c == 7)) for c in range(8)],
                      reads=[wbuf, hbufs[tt // 2] if len(hbufs) * 1024 >= ntok and len(hbufs) > 1 else hbufs[0]],
                      writes=[self.b_ps[r]])
            evac(tt, self.ps[r], self.b_ps[r])

    def program(self, P1, P2, P3, fused):
        S_ = self.S
        nc = self.nc
        self.b_rst = [Buf("rst0"), Buf("rst1")]
        self.load_consts()
        if not P2:
            self.load_x(self.x_in)
        groups = [[0, 1], [2, 3], [4, 5], [6, 7]]

        A_HT = 0
        A_QT = 16 * 1024
        A_MQT = 28 * 1024
        A_KV = 32 * 1024
        A_WS = 66 * 1024
        A_WO = 74 * 1024
        A_STG = 90 * 1024
        A_ACT = 16 * 1024
        A_WGU = 99 * 1024
        A_WDN = 107 * 1024
        A_COS = 32 * 1024
        A_WM = 74 * 1024

        hT = self.aview(A_HT, BF16, [8, 1024])
        b_hT = [Buf("hT0")]
        qT = self.aview(A_QT, BF16, [6, 1024])
        mqT = self.aview(A_MQT, BF16, [2, 1024])
        b_qT = Buf("qT")
        b_mqT = Buf("mqT")
        ws = [self.aview(A_WS + i * 4096, BF16, [8, 256]) for i in range(2)] + \
             [self.aview(A_WGU + i * 4096, BF16, [8, 256]) for i in range(2)]
        b_ws = [Buf("ws0"), Buf("ws1"), Buf("ws2"), Buf("ws3")]
        self._ws_i = 0
        wo = self.aview(A_WO, BF16, [8, 1024])
        b_wo = Buf("wo")

        def ws_next():
            i = self._ws_i
            self._ws_i = (i + 1) % 4
            return ws[i], b_ws[i]

        self._ws128_i = 0

        def ws128_next():
            i = self._ws128_i
            self._ws128_i = (i + 1) % 6
            if i < 4:
                return ws[i], b_ws[i]
            return self.wdn_x[i - 4][:].rearrange("p (c f) -> p c f", c=8), b_wdn[2 + (i - 4)]

        def wview(w, row0, col0, ncol):
            return w[row0:row0 + D, :].rearrange("(c p) f -> p c f", p=128)[:, :, col0:col0 + ncol]

        kstg = [self.aview(A_STG + i * 2048, BF16, [1024]) for i in range(2)]
        b_kstg = [Buf("kstg0"), Buf("kstg1")]
        vstg = [self.aview(A_STG + 4096 + i * 1568, BF16, [784]) for i in range(2)]
        b_vstg = [Buf("vstg0"), Buf("vstg1")]
        lfstg = self.aview(A_STG + 8192, F32, [NB, 12])
        b_lfstg = Buf("lfstg")
        wf = self.aview(A_STG + 7232, BF16, [8, 12])
        b_wf = Buf("wf")
        zt = self.aview(A_STG + 7232 + 192, F32, [2, 12])
        b_zt = Buf("zt")

        def norm_tile(tile, gcol, dst, dbuf):
            self.rms_T_multi([(self.x_sb[:, tile * 8 + nb, :], self.b_x[tile * 8 + nb]) for nb in range(8)], gcol, dst, dbuf,
                             rstd=[self.statx[:, 16 + tile * 8 + nb:17 + tile * 8 + nb] for nb in range(8)])

        def kv_pass(layer, tile):
            gcol = C_G_ATT0 if layer == 0 else C_G_KV
            norm_tile(tile, gcol, hT, b_hT[0])
            w = self.w_a_in if layer == 0 else self.w_kvs
            kcol0 = 768 if layer == 0 else 0
            vcol0 = 1536 if layer == 0 else 768
            kT_b = self.kT0_b if layer == 0 else self.kT1_b
            v_b = self.v0_b if layer == 0 else self.v1_b
            VW = 65 if layer == 0 else 129
            nvh = 12 if layer == 0 else 6
            if layer == 1:
                cosT = self.aview(A_COS, F32, [1024])
                sinT = self.aview(A_COS + 4096, F32, [1024])
                b_cs = b_kT[0]
                S_.dma("sp", cosT, self.cos_in[:, tile * 1024:(tile + 1) * 1024], b_cs, writes=[b_cs])
                S_.dma("sp", sinT, self.sin_in[:, tile * 1024:(tile + 1) * 1024], b_cs, writes=[b_cs])
            for oc in range(6):
                if layer == 0:
                    wt, wb = ws128_next()
                    self.wload(wt[:, :, 0:128], wview(w, 0, kcol0 + oc * 128, 128), wb)
                    i = oc % 2

                    def evac(tt, ps, pbuf, i=i):
                        S_.op("act", ACT(kstg[i][:, tt * 512:(tt + 1) * 512], ps[:, :], AF.Copy),
                              reads=[pbuf], writes=[b_kstg[i]])
                    self.proj_fm(wt, wb, 0, hT, b_hT, 1024, evac)
                else:
                    self.rope_proj(w, kcol0 + oc * 128, hT, b_hT, cosT, sinT, b_cs, ws_next, wview,
                                   lambda tt, i=oc % 2: (kstg[i][:, tt * 512:(tt + 1) * 512], b_kstg[i]))
                    i = oc % 2
                S_.dma("sp", kT_b[tile][oc * 128:(oc + 1) * 128, :], kstg[i], b_kstg[i],
                       reads=[b_kstg[i]])
            wvs = []
            for vc in range(3):
                wt, wb = ws_next()
                self.wload(wt[:, :, 0:256], wview(w, 0, vcol0 + vc * 256, 256), wb)
                wvs.append((wt, wb))
            for nb in range(8):
                n = tile * 8 + nb
                i = nb % 2
                r = self.ring()
                r2 = self.ring()
                fns = []
                for vc in range(3):
                    pst = self.ps[r] if vc < 2 else self.ps[r2]
                    for c in range(8):
                        fns.append(MM(pst[:, (vc % 2) * 256:(vc % 2) * 256 + 256], hT[:, c, nb * 128:(nb + 1) * 128],
                                      wvs[vc][0][:, c, 0:256], start=(c == 0), stop=(c == 7)))
                S_.op("pe", fns, reads=[b_hT[0]] + [wb for _, wb in wvs], writes=[self.b_ps[r], self.b_ps[r2]])
                vv = vstg[i][:, 0:nvh * VW].rearrange("p (h w) -> p h w", h=nvh)
                if nb < 2:
                    S_.op("dve", MS(vstg[i][:, :], 1.0), writes=[b_vstg[i]])
                dv = VW - 1
                S_.op("act", ACT(vv[:, 0:(512 // dv), 0:dv], self.ps[r][:, :].rearrange("p (h w) -> p h w", w=dv), AF.Copy),
                      reads=[self.b_ps[r]], writes=[b_vstg[i]])
                S_.op("act", ACT(vv[:, (512 // dv):nvh, 0:dv], self.ps[r2][:, 0:256].rearrange("p (h w) -> p h w", w=dv), AF.Copy),
                      reads=[self.b_ps[r2]], writes=[b_vstg[i]])
                S_.dma("sp", v_b[tile][nb * 128:(nb + 1) * 128, :], vstg[i][:, 0:nvh * VW], b_vstg[i], reads=[b_vstg[i]])
            if layer == 0:
                self.wload(wf, wview(w, 0, 2304, 12), b_wf)
                for nb in range(8):
                    n = tile * 8 + nb
                    r = self.ring()
                    S_.op("pe", [MM(self.ps[r][:, 0:12], hT[:, c, nb * 128:(nb + 1) * 128], wf[:, c, :],
                                    start=(c == 0), stop=(c == 7)) for c in range(8)],
                          reads=[b_hT[0], b_wf], writes=[self.b_ps[r]])
                    z = zt[:, nb % 2, :]
                    S_.op("dve", TT(z, self.ps[r][:, 0:12], self.cst[:, C_BF:C_BF + 12], ALU.add),
                          reads=[self.b_ps[r], self.b_cst], writes=[b_zt])
                    S_.op("act", ACT(z, z, AF.Exp, scale=-1.0), reads=[b_zt], writes=[b_zt])
                    S_.op("act", ACT(lfstg[:, n, :], z, AF.Ln, bias=self.cst[:, C_ONE:C_ONE + 1]), reads=[b_zt, self.b_cst], writes=[b_lfstg])
                if tile == 1:
                    S_.dma("sp", self.lf_b.rearrange("(n p) h -> p n h", p=128), lfstg, b_lfstg, reads=[b_lfstg])

        kT_sb = [self.aview(A_KV + i * 8192, BF16, [2, 2048]) for i in range(2)]
        b_kT = [Buf("kT0"), Buf("kT1")]
        Vraw = [A_KV + 16384 + i * 8320 for i in range(2)]
        b_V = [Buf("V0"), Buf("V1")]
        actT = self.aview(A_ACT, BF16, [NFC, 1024])
        b_act = Buf("actT")
        wgu = [self.aview(A_WGU + i * 4096, BF16, [8, 2, 128]) for i in range(2)]
        b_wgu = [Buf("wgu0"), Buf("wgu1")]
        wdn = [self.aview(A_WDN + i * 2048, BF16, [1024]) for i in range(2)] + [t[:] for t in self.wdn_x]
        b_wdn = [Buf(f"wdn{i}") for i in range(len(wdn))]
        NWDN = len(wdn)
        b_pT = [[Buf(f"pT{s}_{k}") for k in range(4)] for s in range(4)]
        b_acc = [[Buf(f"acc{b}_{k}") for k in range(4)] for b in range(8)]
        b_mixtm = [Buf(f"mixtm{i}") for i in range(4)]
        b_biasG = [Buf("biasG0"), Buf("biasG1")]
        b_rden = [Buf(f"rden{i}") for i in range(4)]
        b_ytmp = [Buf("ytmp0"), Buf("ytmp1")]
        b_ctab = Buf("ctab")
        b_memK = Buf("memK")
        b_memV = Buf("memV")
        b_fence = Buf("fence")
        b_gate = [[[], []], [[], []]]
        b_gate_lf = []
        self.ropet = [self.aview(Vraw[i], F32, [2, 512]) for i in range(2)]
        self.b_ropet = b_V
        alias_Z = [b_qT, b_mqT, b_kT[0], b_kT[1], b_V[0], b_V[1], b_act]
        cnt = {"pT": 0, "mixtm": 0, "rden": 0, "biasG": 0, "ytmp": 0, "kv": 0, "wgu": 0, "wdn": 0, "grp": 0}

        def nxt(name, n):
            i = cnt[name]
            cnt[name] = (i + 1) % n
            return i

        def fence(bufs):
            S_.op("dve", MS(self.stat[:, 60:61], 0.0), writes=[b_fence] + list(bufs))

        ct = self.ctab
        CSEQ, CREF = 0, 384
        ctmp = self.aview(A_STG, F32, [33 * 12 + 8 * 12 + 32 * 12])
        PRE, TMPD, LFALL = 0, 396, 492

        def ctables():
            LF = ctmp[:, LFALL:LFALL + 384]
            S_.dma("sp", LF.rearrange("p (g h) -> p g h", h=12), self.lf_g.rearrange("(g p) h -> p g h", p=128),
                   b_ctab, reads=b_gate_lf, writes=[b_ctab] + b_kstg + b_vstg + [b_lfstg])
            r1 = self.ring()
            r2 = self.ring()
            S_.op("pe", MM(self.ps[r1][:, 0:384], self.cf[:, 0:128], LF), reads=[b_ctab, self.b_cf], writes=[self.b_ps[r1]])
            S_.op("pe", MM(self.ps[r2][:, 0:384], self.cf[:, 128:256], LF), reads=[b_ctab, self.b_cf], writes=[self.b_ps[r2]])
            S_.op("dve", MS(ctmp[:, PRE:PRE + 12], 0.0), writes=[b_ctab])
            for j in range(32):
                r, n = gidx(j)
                g = r * 16 + n
                S_.op("dve", TT(ctmp[:, PRE + (j + 1) * 12:PRE + (j + 2) * 12], ctmp[:, PRE + j * 12:PRE + (j + 1) * 12],
                                self.ps[r2][:, g * 12:(g + 1) * 12], ALU.add), reads=[self.b_ps[r2]], writes=[b_ctab])
            cw = self.ps[r1][:, 0:384].rearrange("p (r m e h) -> p r m e h", r=2, m=8, e=2)
            pre_v = ctmp[:, PRE:PRE + 384].rearrange("p (m q h) -> p m q h", m=8, q=4)
            cs_v = ct[:, CSEQ:CSEQ + 384].rearrange("p (m q h) -> p m q h", m=8, q=4)
            for q, (r, e) in enumerate(((0, 0), (1, 0), (1, 1), (0, 1))):
                S_.op("dve", TT(cs_v[:, :, q, :], pre_v[:, :, q, :], cw[:, r, :, e, :], ALU.add),
                      reads=[self.b_ps[r1]], writes=[b_ctab])
            inc_v = ctmp[:, PRE + 12:PRE + 12 + 384].rearrange("p (m q h) -> p m q h", m=8, q=4)
            ref_v = ct[:, CREF:CREF + 192].rearrange("p (m e h) -> p m e h", m=8, e=2)
            tmp_v = ctmp[:, TMPD:TMPD + 96].rearrange("p (m h) -> p m h", m=8)
            par = self.cst[:, C_PAR:C_PAR + 1]
            for e, (qa, qb) in enumerate(((0, 1), (3, 2))):
                S_.op("dve", TT(tmp_v, inc_v[:, :, qb, :], inc_v[:, :, qa, :], ALU.subtract), writes=[b_ctab])
                S_.op("dve", STT(ref_v[:, :, e, :], tmp_v, par, inc_v[:, :, qa, :], ALU.mult, ALU.add),
                      reads=[self.b_cst], writes=[b_ctab])

        def mem_kv(layer):
            mem_sb = self.aview(A_KV, F32, [2, 1024])
            hmT = self.aview(A_KV + 8192, BF16, [8, 256])
            wm = self.aview(A_WO, BF16, [8, 512])
            S_.dma("sp", mem_sb, self.mem_in.rearrange("(m p) d -> p m d", p=128), b_kT[0], writes=[b_kT[0]])
            self.wload(wm, wview(self.w_mem, layer * D, 0, 512), b_wo)
            self.rms_T_multi([(mem_sb[:, mb, :], b_kT[0]) for mb in range(2)], C_G_MEM0 if layer == 0 else C_G_MEM1, hmT, b_kT[1])
            for pm in range(2):
                r = self.ring()
                S_.op("pe", [MM(self.ps[r][:, 0:256], wm[:, c, pm * 128:(pm + 1) * 128], hmT[:, c, :],
                                start=(c == 0), stop=(c == 7)) for c in range(8)],
                      reads=[b_wo, b_kT[1]], writes=[self.b_ps[r]])
                S_.op("act", ACT(self.memKT[:, pm, :], self.ps[r][:, 0:256], AF.Copy), reads=[self.b_ps[r]], writes=[b_memK])
            S_.op("dve", MS(self.memV[:], 1.0), writes=[b_memV])
            for mb in range(2):
                r = self.ring()
                S_.op("pe", [MM(self.ps[r][:, 0:256], hmT[:, c, mb * 128:(mb + 1) * 128], wm[:, c, 256:512],
                                start=(c == 0), stop=(c == 7)) for c in range(8)],
                      reads=[b_wo, b_kT[1]], writes=[self.b_ps[r]])
                S_.op("act", ACT(self.memV[:, mb, :, 0:64], self.ps[r][:, 0:256].rearrange("p (h w) -> p h w", h=4), AF.Copy),
                      reads=[self.b_ps[r]], writes=[b_memV])

        def attend(steps, tile, units, Jof, masked, W, finish):
            for gl in range(2):
                ns = [tile * 8 + gl * 4 + nn for nn in range(4)]
                Jmax = Jof(ns[-1])
                gpar = nxt("grp", 2)
                if W == 65:
                    accap = lambda a, nn, gpar=gpar: (4 + a + 2 * gpar, nn, self.ps[4 + a + 2 * gpar][:, nn * 65:(nn + 1) * 65])
                else:
                    accap = lambda a, nn: (4 + 2 * a + nn // 2, nn % 2,
                                           self.ps[4 + 2 * a + nn // 2][:, (nn % 2) * 129:(nn % 2) * 129 + 129])
                ub = {}
                Jfar = max(0, Jof(ns[0]) - 2) if units[0].get("bias") else 0
                for j in range(Jmax):
                    nn0 = min(nn for nn in range(4) if Jof(ns[nn]) > j)
                    cn = 4 - nn0
                    for a, u in enumerate(units):
                        st = {}

                        def front(st=st, a=a, u=u, j=j, nn0=nn0, cn=cn, ns=ns, gl=gl, ub=ub, Jfar=Jfar):
                            if j == 0 and u.get("prep"):
                                ub[a] = u["prep"](gl, ns)
                                if Jfar > 0:
                                    if a == 0:
                                        ub["fi"] = nxt("rden", 4)
                                    fi = ub["fi"]
                                    h_ = u["head"]
                                    crv = ct[:, CREF:CREF + 192].rearrange("p (n h) -> p n h", h=12)[:, ns[1]:ns[3] + 1, h_]
                                    fv = self.rden[fi][:, a * 4 + 1:a * 4 + 4]
                                    S_.op("dve", TS(fv, crv, ct[:, CREF + ns[0] * 12 + h_:CREF + ns[0] * 12 + h_ + 1], None, ALU.subtract),
                                          reads=[b_ctab], writes=[b_rden[fi]])
                                    S_.op("act", ACT(fv, fv, AF.Exp, scale=-1.0), writes=[b_rden[fi]])
                            r = self.ring()
                            kap, kb = u["k"](j)
                            qap, qb = u["q"]((gl * 4 + nn0) * 128, cn * 128)
                            S_.op("pe", MM(self.ps[r][:, 0:cn * 128], kap, qap), reads=[kb, qb], writes=[self.b_ps[r]])
                            s_ = nxt("pT", 4)
                            st["s"] = s_
                            if u.get("bias") and j < Jfar:
                                bap, bb = u["bias"](ub[a], 0, j)
                                S_.op("act", ACT(self.pT[s_][:, 0:512], self.ps[r][:, 0:512], AF.Exp, bias=bap, scale=0.125),
                                      reads=[self.b_ps[r], bb], writes=b_pT[s_])
                            elif u.get("bias"):
                                for k in range(cn):
                                    bap, bb = u["bias"](ub[a], nn0 + k, j)
                                    S_.op("act", ACT(self.pT[s_][:, k * 128:(k + 1) * 128], self.ps[r][:, k * 128:(k + 1) * 128],
                                                     AF.Exp, bias=bap, scale=0.125),
                                          reads=[self.b_ps[r], bb], writes=[b_pT[s_][k]])
                            else:
                                S_.op("act", ACT(self.pT[s_][:, 0:cn * 128], self.ps[r][:, 0:cn * 128], AF.Exp, scale=0.125),
                                      reads=[self.b_ps[r]], writes=b_pT[s_][0:cn])
                            for k in range(cn):
                                n = ns[nn0 + k]
                                if masked and j >= Jof(n) - 2:
                                    mi = 1 + 2 * (n % 2) + (j - (Jof(n) - 2))
                                    S_.op("pool", TT(self.pT[s_][:, k * 128:(k + 1) * 128], self.pT[s_][:, k * 128:(k + 1) * 128],
                                                     self.cbf[:, mi * 128:(mi + 1) * 128], ALU.mult),
                                          reads=[self.b_cbf], writes=[b_pT[s_][k]])

                        def back(st=st, a=a, u=u, j=j, nn0=nn0, cn=cn, ns=ns, gl=gl, accap=accap, ub=ub, Jfar=Jfar,
                                 last=(j == Jmax - 1 and a == len(units) - 1)):
                            s_ = st["s"]
                            vap, vb = u["v"](j)
                            for k in range(cn):
                                nn = nn0 + k
                                bank, sub, aap = accap(a, nn)
                                S_.op("pe", MM(aap, self.pT[s_][:, k * 128:(k + 1) * 128], vap,
                                               start=(j == 0 and sub == 0), stop=(j == Jof(ns[nn]) - 1), skip=True),
                                      reads=[b_pT[s_][k], vb], writes=[b_acc[bank][sub]])
                            if Jfar > 0 and j == Jfar - 1:
                                fi = ub["fi"]
                                for nn in range(1, 4):
                                    bank, sub, aap = accap(a, nn)
                                    S_.op("dve", TS(aap, aap, self.rden[fi][:, a * 4 + nn:a * 4 + nn + 1], None, ALU.mult),
                                          reads=[b_rden[fi]], writes=b_acc[bank])
                            if last:
                                finish(gl, ns, accap)
                        steps.append((front, back))

        deferred = []
        cur_pair = [0]

        def put_mixT(slot, chunk, tokblk):
            deferred.append([cur_pair[0] + 2, lambda: put_mixT_now(slot, chunk, tokblk)])

        def run_deferred(flush=False):
            for item in list(deferred):
                if flush or item[0] <= cur_pair[0]:
                    item[1]()
                    deferred.remove(item)

        def put_mixT_now(slot, chunk, tokblk):
            r = self.ring()
            pb = self.ps[r][:, 0:64].bitcast(BF16)
            S_.op("pe", TR(pb, self.mixtm[slot][:], self.ident), reads=[b_mixtm[slot], self.b_cbf], writes=[self.b_ps[r]])
            S_.op("dve", CP(hT[:, chunk, tokblk * 128:(tokblk + 1) * 128], pb), reads=[self.b_ps[r]], writes=[b_hT[0]])

        def finish65(chunk):
            def fin(gl, ns, accap):
                slots = [nxt("mixtm", 4) for _ in range(4)]
                for a in range(2):
                    ri = nxt("rden", 4)
                    bank = accap(a, 0)[0]
                    den = self.ps[bank][:, 0:260].rearrange("p (n w) -> p n w", w=65)[:, :, 64:65]
                    S_.op("dve", RCP(self.rden[ri][:, 0:4].unsqueeze(2), den), reads=b_acc[bank], writes=[b_rden[ri]])
                    for nn in range(4):
                        _, sub, aap = accap(a, nn)
                        S_.op("dve", TS(self.mixtm[slots[nn]][:, a * 64:(a + 1) * 64], aap[:, 0:64],
                                        self.rden[ri][:, nn:nn + 1], None, ALU.mult),
                              reads=b_acc[bank] + [b_rden[ri]], writes=[b_mixtm[slots[nn]]])
                for nn in range(4):
                    put_mixT(slots[nn], chunk, gl * 4 + nn)
            return fin

        def finish_diff(chunk):
            def fin(gl, ns, accap):
                ri = nxt("rden", 4)
                ri2 = nxt("rden", 4)
                rd = self.rden[ri]
                rd2 = self.rden[ri2]
                for a in range(2):
                    for hb in range(2):
                        bank = 4 + 2 * a + hb
                        den = self.ps[bank][:, 0:258].rearrange("p (n w) -> p n w", w=129)[:, :, 128:129]
                        S_.op("dve", RCP(rd[:, 4 * a + 2 * hb:4 * a + 2 * hb + 2].unsqueeze(2), den),
                              reads=b_acc[bank], writes=[b_rden[ri]])
                S_.op("dve", TS(rd[:, 4:8], rd[:, 4:8], self.stat[:, 16:17], None, ALU.mult), reads=[self.b_stat], writes=[b_rden[ri]])
                ys = []
                sls = []
                for nn in range(4):
                    yi = nxt("ytmp", 4)
                    sl = nxt("mixtm", 4)
                    b1, s1, a1 = accap(0, nn)
                    b2, s2, a2 = accap(1, nn)
                    y = self.ytmp[yi // 2][:, yi % 2, :]
                    S_.op("act", ACT(y, a2[:, 0:128], AF.Copy, scale=rd[:, 4 + nn:5 + nn]),
                          reads=b_acc[b2] + [b_rden[ri]], writes=[b_ytmp[yi // 2]])
                    S_.op("dve", STT(y, a1[:, 0:128], rd[:, nn:nn + 1], y, ALU.mult, ALU.add),
                          reads=b_acc[b1] + [b_rden[ri]], writes=[b_ytmp[yi // 2]])
                    ys.append((y, yi))
                    sls.append(sl)
                for nn in range(4):
                    y, yi = ys[nn]
                    sl = sls[nn]
                    S_.op("dve", lambda e, y=y, sl=sl, nn=nn, rd2=rd2: e.scalar_tensor_tensor(
                        self.mixtm[sl][:], y, 1.0, y, ALU.mult, ALU.mult, accum_out=rd2[:, nn:nn + 1]),
                        reads=[b_ytmp[yi // 2]], writes=[b_mixtm[sl], b_rden[ri2]])
                S_.op("act", ACT(rd2[:, 4:8], rd2[:, 0:4], AF.Ln, bias=self.cst[:, C_EPS:C_EPS + 1], scale=1.0 / 128),
                      reads=[self.b_cst], writes=[b_rden[ri2]])
                S_.op("act", ACT(rd2[:, 4:8], rd2[:, 4:8], AF.Exp, scale=-0.5), writes=[b_rden[ri2]])
                for nn in range(4):
                    y, yi = ys[nn]
                    S_.op("dve", STT(self.mixtm[sls[nn]][:], y, rd2[:, 4 + nn:5 + nn], self.cst[:, C_SUBG:C_SUBG + 128], ALU.mult, ALU.mult),
                          reads=[b_ytmp[yi // 2], b_rden[ri2], self.b_cst], writes=[b_mixtm[sls[nn]]])
                    put_mixT(sls[nn], chunk, gl * 4 + nn)
            return fin

        def load_kv(layer, tile, hp):
            i = nxt("kv", 2)
            ntok = 1024 * (tile + 1)
            kg = self.kT0_g if layer == 0 else self.kT1_g
            vg = self.v0_g if layer == 0 else self.v1_g
            VWp = 130 if layer == 0 else 129
            Vv = self.aview(Vraw[i], BF16, [2, 16, VWp])
            for t in range(tile + 1):
                for r in range(2):
                    S_.dma("sp", kT_sb[i][:, r, t * 1024:(t + 1) * 1024], kg[t][r * 768 + hp * 128:r * 768 + (hp + 1) * 128, :],
                           b_kT[i], writes=[b_kT[i]])
                    S_.dma("sp", Vv[:, r, t * 8:(t + 1) * 8, :],
                           vg[t].rearrange("(r n p) w -> p r n w", r=2, p=128)[:, r, :, hp * VWp:(hp + 1) * VWp],
                           b_V[i], writes=[b_V[i]])
            return i, Vv

        def load_kv_into(layer, tile, hp, i):
            kg = self.kT0_g if layer == 0 else self.kT1_g
            vg = self.v0_g if layer == 0 else self.v1_g
            VWp = 130 if layer == 0 else 129
            Vv = self.aview(Vraw[i], BF16, [2, 16, VWp])
            for t in range(tile + 1):
                for r in range(2):
                    S_.dma("sp", kT_sb[i][:, r, t * 1024:(t + 1) * 1024], kg[t][r * 768 + hp * 128:r * 768 + (hp + 1) * 128, :],
                           b_kT[i], reads=b_gate[layer][t], writes=[b_kT[i]])
                    S_.dma("sp", Vv[:, r, t * 8:(t + 1) * 8, :],
                           vg[t].rearrange("(r n p) w -> p r n w", r=2, p=128)[:, r, :, hp * VWp:(hp + 1) * VWp],
                           b_V[i], reads=b_gate[layer][t], writes=[b_V[i]])

        def attention(layer, tile, DEPTH=2):
            nhp = 6
            VWp = 130 if layer == 0 else 129
            steps = []
            slots = [nxt("kv", 2) for _ in range(nhp)]
            load_kv_into(layer, tile, 0, slots[0])
            load_kv_into(layer, tile, 1, slots[1])
            for hp in range(nhp):
                i = slots[hp]
                Vv = self.aview(Vraw[i], BF16, [2, 16, VWp])
                units = []
                for a in range(2):
                    rows = slice(a * 64, (a + 1) * 64)

                    def kf(j, rows=rows, i=i):
                        r, n = gidx(j)
                        return kT_sb[i][rows, r, n * 128:(n + 1) * 128], b_kT[i]

                    def qf(t0, nt, rows=rows, hp=hp):
                        return qT[rows, hp, t0:t0 + nt], b_qT

                    if layer == 0:
                        def vf(j, a=a, i=i, Vv=Vv):
                            r, n = gidx(j)
                            return Vv[:, r, n, a * 65:(a + 1) * 65], b_V[i]
                        h = 2 * hp + a

                        def prep(gl, ns, a=a, h=h):
                            if a == 0:
                                self._bg = nxt("biasG", 2)
                            bg = self._bg
                            for nn, n in enumerate(ns):
                                Jn = JN(n)
                                cin = ct[:, CSEQ:CSEQ + 384].rearrange("p (j h) -> p j h", h=12)[:, 0:Jn, h]
                                S_.op("dve", TS(self.biasG[bg][:, a, nn, 0:Jn], cin, ct[:, CREF + n * 12 + h:CREF + n * 12 + h + 1],
                                                None, ALU.subtract), reads=[b_ctab], writes=[b_biasG[bg]])
                            return bg

                        def bf(bg, nn, j, a=a):
                            return self.biasG[bg][:, a, nn, j:j + 1], b_biasG[bg]
                        units.append(dict(q=qf, k=kf, v=vf, bias=bf, prep=prep, head=h))
                    else:
                        def vf(j, i=i, Vv=Vv):
                            r, n = gidx(j)
                            return Vv[:, r, n, :], b_V[i]
                        units.append(dict(q=qf, k=kf, v=vf))
                n0 = len(steps)
                if layer == 0:
                    attend(steps, tile, units, JN, True, 65, finish65(hp))
                else:
                    attend(steps, tile, units, JN, True, 129, finish_diff(hp))
                if hp + 2 < nhp:
                    f_, b_ = steps[-1]

                    def back2(b_=b_, hp=hp):
                        b_()
                        load_kv_into(layer, tile, hp + 2, slots[hp + 2])
                    steps[-1] = (f_, back2)
            for pm in range(2):
                units = []
                for a in range(2):
                    rows = slice(a * 64, (a + 1) * 64)

                    def kf(j, rows=rows, pm=pm):
                        return self.memKT[rows, pm, j * 128:(j + 1) * 128], b_memK

                    def qf(t0, nt, rows=rows, pm=pm):
                        return mqT[rows, pm, t0:t0 + nt], b_mqT

                    def vf(j, a=a, pm=pm):
                        return self.memV[:, j, 2 * pm + a, :], b_memV
                    units.append(dict(q=qf, k=kf, v=vf))
                attend(steps, tile, units, lambda n: 2, False, 65, finish65(6 + pm))
            assert len(steps) % 2 == 0
            npair = len(steps) // 2
            for p in range(npair + 2):
                cur_pair[0] = p
                run_deferred()
                if 0 <= 2 * p - 3 < len(steps):
                    steps[2 * p - 3][1]()
                if p < npair:
                    steps[2 * p][0]()
                    steps[2 * p + 1][0]()
                if 0 <= 2 * p - 2 < len(steps):
                    steps[2 * p - 2][1]()
            run_deferred(flush=True)
            cur_pair[0] = 0

        def q_proj(layer, tile):
            w = self.w_a_in if layer == 0 else self.w_b_in
            if layer == 1:
                cosT = self.aview(A_COS, F32, [1024])
                sinT = self.aview(A_COS + 4096, F32, [1024])
                b_cs = b_kT[0]
                S_.dma("sp", cosT, self.cos_in[:, tile * 1024:(tile + 1) * 1024], b_cs, writes=[b_cs])
                S_.dma("sp", sinT, self.sin_in[:, tile * 1024:(tile + 1) * 1024], b_cs, writes=[b_cs])
            for oc in range(8):
                dst, dbuf = (qT, b_qT) if oc < 6 else (mqT, b_mqT)
                dc = oc if oc < 6 else oc - 6
                col0 = (oc * 128 if oc < 6 else (2316 + dc * 128)) if layer == 0 else oc * 128
                if layer == 1 and oc < 6:
                    self.rope_proj(w, col0, hT, b_hT, cosT, sinT, b_cs, ws_next, wview,
                                   lambda tt, dc=dc: (qT[:, dc, tt * 512:(tt + 1) * 512], b_qT))
                else:
                    wt, wb = ws128_next()
                    self.wload(wt[:, :, 0:128], wview(w, 0, col0, 128), wb)

                    def evac(tt, ps, pbuf, dst=dst, dbuf=dbuf, dc=dc):
                        S_.op("act", ACT(dst[:, dc, tt * 512:(tt + 1) * 512], ps[:, :], AF.Copy), reads=[pbuf], writes=[dbuf])
                    self.proj_fm(wt, wb, 0, hT, b_hT, 1024, evac)

        def out_proj(layer, tile):
            for nb in range(8):
                n = tile * 8 + nb
                for half in range(2):
                    r = self.ring()
                    S_.op("pe", [MM(self.ps[r][:, :], hT[:, c, nb * 128:(nb + 1) * 128], wo[:, c, half * 512:(half + 1) * 512],
                                    start=(c == 0), stop=(c == 7)) for c in range(8)],
                          reads=[b_hT[0], b_wo], writes=[self.b_ps[r]])
                    xs = self.x_sb[:, n, half * 512:(half + 1) * 512]
                    S_.op("dve", TT(xs, xs, self.ps[r][:, :], ALU.add), reads=[self.b_ps[r]], writes=[self.b_x[n]])
                self.x_sq(n)
            self.x_rstd(tile)

        def ffn(layer, tile):
            norm_tile(tile, C_G_FFN0 if layer == 0 else C_G_FFN1, hT, b_hT[0])
            fence(alias_Z)
            for f in range(NFC):
                wgt, wgb = ws_next()
                wgt = wgt.rearrange("p c (g f) -> p c g f", g=2)
                self.wload(wgt[:, :, 0, :], wview(self.w_gu, layer * D, f * 128, 128), wgb)
                self.wload(wgt[:, :, 1, :], wview(self.w_gu, layer * D, DFF + f * 128, 128), wgb)
                for tt in range(2):
                    rg = self.ring()
                    ru = self.ring()
                    S_.op("pe", [MM(self.ps[rg][:, :], wgt[:, c, 0, :], hT[:, c, tt * 512:(tt + 1) * 512],
                                    start=(c == 0), stop=(c == 7)) for c in range(8)],
                          reads=[wgb, b_hT[0]], writes=[self.b_ps[rg]])
                    S_.op("pe", [MM(self.ps[ru][:, :], wgt[:, c, 1, :], hT[:, c, tt * 512:(tt + 1) * 512],
                                    start=(c == 0), stop=(c == 7)) for c in range(8)],
                          reads=[wgb, b_hT[0]], writes=[self.b_ps[ru]])
                    s = nxt("pT", 4)
                    S_.op("act", ACT(self.pT[s][:], self.ps[rg][:, :], AF.Silu), reads=[self.b_ps[rg]], writes=b_pT[s])
                    S_.op("dve", TT(actT[:, f, tt * 512:(tt + 1) * 512], self.pT[s][:], self.ps[ru][:, :], ALU.mult),
                          reads=b_pT[s] + [self.b_ps[ru]], writes=[b_act])
            for tt in range(2):
                for f in range(NFC):
                    wi = nxt("wdn", NWDN)
                    self.wload(wdn[wi], self.w_dn[layer * DFF + f * 128:layer * DFF + (f + 1) * 128, :], b_wdn[wi])
                    fns = []
                    for nb in range(4):
                        for half in range(2):
                            fns.append(MM(self.ps[nb * 2 + half][:, :], actT[:, f, tt * 512 + nb * 128:tt * 512 + (nb + 1) * 128],
                                          wdn[wi][:, half * 512:(half + 1) * 512], start=(f == 0), stop=(f == NFC - 1)))
                    if f == 0:
                        for bi, fn in enumerate(fns):
                            S_.op("pe", fn, reads=[b_act, b_wdn[wi]], writes=[self.b_ps[bi]])
                    else:
                        S_.op("pe", fns, reads=[b_act, b_wdn[wi]], writes=self.b_ps)
                for nb in range(4):
                    n = tile * 8 + tt * 4 + nb
                    for half in range(2):
                        xs = self.x_sb[:, n, half * 512:(half + 1) * 512]
                        S_.op("dve", TT(xs, xs, self.ps[nb * 2 + half][:, :], ALU.add),
                              reads=[self.b_ps[nb * 2 + half]], writes=[self.b_x[n]])
                    self.x_sq(n)
            self.x_rstd(tile)
            fence(alias_Z)

        def final_norm(tile):
            for nb in range(8):
                n = tile * 8 + nb
                i = self._rms_i = (getattr(self, "_rms_i", 0) + 1) % 2
                ss = self.stat[:, 2 * i:2 * i + 1]
                rstd = self.stat[:, 2 * i + 1:2 * i + 2]
                bst = self.b_rst[i]
                xs = self.x_sb[:, n, :]
                rstd = self.statx[:, 16 + n:17 + n]
                S_.op("dve", STT(xs, xs, rstd, self.cst[:, C_FING:C_FING + 1024], ALU.mult, ALU.mult),
                      reads=[self.b_statx, self.b_cst], writes=[self.b_x[n]])
                S_.dma("sp", self.out_d[n * 128:(n + 1) * 128, :], xs, self.b_x[n], reads=[self.b_x[n]])

        def lam_setup():
            t = self.ytmp[0]
            for k in range(2):
                S_.op("dve", TT(t[:, k, 0:64], self.cst[:, C_LAM + 128 * k:C_LAM + 128 * k + 64],
                                self.cst[:, C_LAM + 128 * k + 64:C_LAM + 128 * k + 128], ALU.mult),
                      reads=[self.b_cst], writes=[b_ytmp[0]])
                S_.op("dve", lambda e, k=k: e.tensor_reduce(self.stat[:, 12 + k:13 + k], t[:, k, 0:64],
                                                             mybir.AxisListType.X, ALU.add),
                      reads=[b_ytmp[0]], writes=[self.b_stat])
                S_.op("act", ACT(self.stat[:, 14 + k:15 + k], self.stat[:, 12 + k:13 + k], AF.Exp), writes=[self.b_stat])
            S_.op("dve", TS(self.cst[:, C_SUBG:C_SUBG + 128], self.cst[:, C_SUBG:C_SUBG + 128], 1.0 - LAM_INIT, None, ALU.mult),
                  writes=[self.b_cst])
            S_.op("dve", TS(self.stat[:, 16:17], self.stat[:, 15:16], self.stat[:, 14:15], -LAM_INIT, ALU.subtract, ALU.add),
                  writes=[self.b_stat])

        def exchange(layer, tile):
            kb, kg = (self.kT0_b, self.kT0_g) if layer == 0 else (self.kT1_b, self.kT1_g)
            vb, vg = (self.v0_b, self.v0_g) if layer == 0 else (self.v1_b, self.v1_g)
            items = [(f"kT{layer}{tile}", kb[tile], kg[tile], b_kstg), (f"v{layer}{tile}", vb[tile], vg[tile], b_vstg)]
            if layer == 0 and tile == 1:
                items += [("lf0", self.lf_b, self.lf_g, [b_lfstg])]
            for name, own, full, bufs in items:
                gt = Buf("gate_" + name)
                S_.collective(own, full, groups, name, reads=[], writes=[gt], after=list(bufs))
                if name == "lf0":
                    b_gate_lf.append(gt)
                else:
                    b_gate[layer][tile].append(gt)

        def dump_x(dst):
            for n in range(NB):
                S_.dma("sp", dst[n * 128:(n + 1) * 128, :], self.x_sb[:, n, :], self.b_x[n], reads=[self.b_x[n]])

        def load_wo(layer):
            self.wload(wo, wview(self.w_out, layer * D, 0, 1024), b_wo)

        dbg = self.dbg
        if P2:
            mem_kv(0)
            self.load_x(self.x_in)
            for t_ in range(2):
                for nb_ in range(8):
                    self.x_sq(t_ * 8 + nb_)
                self.x_rstd(t_)
        if P1:
            for tile in range(2):
                kv_pass(0, tile)
                if fused:
                    exchange(0, tile)
        if P2:
            fence(alias_Z)
            norm_tile(0, C_G_ATT0, hT, b_hT[0])
            q_proj(0, 0)
            ctables()
            if dbg == "ct":
                S_.dma("sp", self.dbg_s[:, 0:1612], ct[:, 0:1612], b_ctab, reads=[b_ctab])
                return
            if dbg == "mem":
                S_.dma("pool", self.dbg_s[:, 0:512], self.memKT[:].rearrange("p a b -> p (a b)"), b_memK, reads=[b_memK])
                S_.dma("pool", self.dbg_s[:, 512:512 + 520], self.memV[:].rearrange("p a b c -> p (a b c)"), b_memV, reads=[b_memV])
                return
            for tile in range(2):
                if tile > 0:
                    fence(alias_Z)
                    norm_tile(tile, C_G_ATT0, hT, b_hT[0])
                    q_proj(0, tile)
                if tile == 0:
                    load_wo(0)
                if dbg == "qp":
                    S_.dma("pool", self.dbg_s[:, 0:6144], qT.rearrange("p a b -> p (a b)"), b_qT, reads=[b_qT])
                    S_.dma("pool", self.dbg_s[:, 6144:8192], mqT.rearrange("p a b -> p (a b)"), b_mqT, reads=[b_mqT])
                    return
                attention(0, tile)
                if dbg == "attn0":
                    break
                out_proj(0, tile)
                if dbg == "op0":
                    dump_x(self.dbg_d)
                    return
                ffn(0, tile)
                if dbg == "ffn0":
                    dump_x(self.dbg_d)
                    return
                kv_pass(1, tile)
                if fused:
                    exchange(1, tile)
                if dbg == "kv10":
                    dump_x(self.dbg_d)
                    return
            if dbg == "attn0":
                S_.dma("pool", self.dbg_m.rearrange("p (c f) -> p c f", c=8), hT, b_hT[0], reads=[b_hT[0]])
                return
            if not fused:
                dump_x(self.xmid_d)
        if dbg == "L0":
            dump_x(self.dbg_d)
            return
        if P3:
            lam_setup()
            for tile in range(2):
                fence(alias_Z)
                norm_tile(tile, C_G_ATT1, hT, b_hT[0])
                q_proj(1, tile)
                if tile == 0:
                    mem_kv(1)
                    load_wo(1)
                attention(1, tile)
                out_proj(1, tile)
                ffn(1, tile)
                final_norm(tile)

    def rope_proj(self, w, c0, hT, b_hT, cosT, sinT, b_cs, ws_next, wview, dstf, tmp=None):
        S_ = self.S
        wt, wb = ws_next()
        self.wload(wt[:, :, 0:128], wview(w, 0, c0, 128), wb)
        src_v = wt[:, :, 0:128].rearrange("p c (b h f) -> p c b h f", b=2, h=2)
        dst_v = wt[:, :, 128:256].rearrange("p c (b h f) -> p c b h f", b=2, h=2)
        for hf in range(2):
            S_.op("act", ACT(dst_v[:, :, :, hf, :], src_v[:, :, :, 1 - hf, :], AF.Copy), reads=[wb], writes=[wb])
        for tt in range(2):
            ra = self.ring()
            rb = self.ring()
            S_.op("pe", [MM(self.ps[ra][:, :], wt[:, c, 0:128], hT[:, c, tt * 512:(tt + 1) * 512], start=(c == 0), stop=(c == 7))
                         for c in range(8)], reads=[wb, b_hT[0]], writes=[self.b_ps[ra]])
            S_.op("pe", [MM(self.ps[rb][:, :], wt[:, c, 128:256], hT[:, c, tt * 512:(tt + 1) * 512], start=(c == 0), stop=(c == 7))
                         for c in range(8)], reads=[wb, b_hT[0]], writes=[self.b_ps[rb]])
            i = self._rt_i = (getattr(self, "_rt_i", 0) + 1) % 2
            ta = self.ropet[i][:, 0, :]
            tb2 = self.ropet[i][:, 1, :]
            S_.op("dve", TT(ta, self.ps[ra][:, :], cosT[:, tt * 512:(tt + 1) * 512], ALU.mult),
                  reads=[self.b_ps[ra], b_cs], writes=[self.b_ropet[i]])
            S_.op("dve", TT(tb2, self.ps[rb][:, :], sinT[:, tt * 512:(tt + 1) * 512], ALU.mult),
                  reads=[self.b_ps[rb], b_cs], writes=[self.b_ropet[i]])
            dap, dbuf = dstf(tt)
            S_.op("dve", TT(dap, ta, tb2, ALU.add), reads=[self.b_ropet[i]], writes=[dbuf])


def build_nc(mode="ALL", dbg=None):
    return Builder(mode, dbg).build()


def _bf(a):
    return np.ascontiguousarray(a).astype(ml_dtypes.bfloat16)


def own_rows(p):
    return np.concatenate([np.arange(seq_block(p, n) * 128, seq_block(p, n) * 128 + 128) for n in range(NB)])


def host_consts(inputs, p):
    f = lambda a: np.asarray(a, np.float32)
    cst = np.zeros((128, NCST), np.float32)

    def gT(v):
        return f(v).reshape(8, 128).T

    cst[:, C_G_ATT0:C_G_ATT0 + 8] = gT(inputs["attn_norm_g"][0])
    cst[:, C_G_ATT1:C_G_ATT1 + 8] = gT(inputs["attn_norm_g"][1])
    cst[:, C_G_FFN0:C_G_FFN0 + 8] = gT(inputs["ffn_norm_g"][0])
    cst[:, C_G_FFN1:C_G_FFN1 + 8] = gT(inputs["ffn_norm_g"][1])
    cst[:, C_G_KV:C_G_KV + 8] = gT(inputs["kv_norm_g"])
    cst[:, C_G_MEM0:C_G_MEM0 + 8] = gT(inputs["mem_norm_g"][0])
    cst[:, C_G_MEM1:C_G_MEM1 + 8] = gT(inputs["mem_norm_g"][1])
    cst[:, C_BF:C_BF + 12] = f(inputs["a_b_f"][0])[None, :]
    cst[:, C_PAR] = float(p)
    for i, k in enumerate(("b_lambda_q1", "b_lambda_k1", "b_lambda_q2", "b_lambda_k2")):
        cst[:, C_LAM + 64 * i:C_LAM + 64 * (i + 1)] = f(inputs[k][0])[None, :]
    cst[:, C_SUBG:C_SUBG + 128] = f(inputs["b_subln_g"][0])[None, :]
    cst[:, C_FING:C_FING + 1024] = f(inputs["final_norm_g"])[None, :]
    cst[:, C_ONE] = 1.0
    cst[:, C_EPS] = EPS
    k = np.arange(128)[:, None]
    q = np.arange(128)[None, :]
    tri = (q >= k).astype(np.float32)
    ones = np.ones((128, 128), np.float32)
    zeros = np.zeros((128, 128), np.float32)
    if p == 0:
        masks = [tri, zeros, ones, tri]
    else:
        masks = [ones, tri, tri, zeros]
    cbf = _bf(np.concatenate([np.eye(128, dtype=np.float32)] + masks, axis=1))
    cf32 = np.concatenate([(k <= q).astype(np.float32), ones], axis=1)
    pos = own_rows(p).astype(np.float64)
    inv = np.power(10000.0, -np.arange(32, dtype=np.float64) * (2.0 / 64))
    ang = pos[None, :] * inv[np.arange(128) % 32][:, None]
    cosT = np.cos(ang).astype(np.float32)
    sgn = np.where((np.arange(128) % 64) < 32, -1.0, 1.0)[:, None]
    sinT = (np.sin(ang) * sgn).astype(np.float32)
    return cst, cbf, cf32, cosT, sinT


def core_inputs(inputs, c, x_src=None):
    b, p = divmod(c, 2)
    f = lambda a: np.ascontiguousarray(np.asarray(a, np.float32))
    cst, cbf, cf32, cosT, sinT = host_consts(inputs, p)
    xs = f(inputs["x"][b]) if x_src is None else x_src[b]
    return {
        "x_own": np.ascontiguousarray(xs[own_rows(p)]),
        "mem_b": f(inputs["mem"][b]),
        "cst": cst, "cbf": cbf, "cf32": cf32, "cosT": cosT, "sinT": sinT,
        "a_w_in": f(inputs["a_w_in"][0]),
        "b_w_in": f(inputs["b_w_in"][0]),
        "w_out": f(inputs["w_out"]).reshape(2 * D, D),
        "w_gate_up": f(inputs["w_gate_up"]).reshape(2 * D, 2 * DFF),
        "w_down": f(inputs["w_down"]).reshape(2 * DFF, D),
        "w_mem_kv": f(inputs["w_mem_kv"]).reshape(2 * D, 512),
        "w_kv_shared": f(inputs["w_kv_shared"]),
    }


def kernel(**inputs):
    nc = build_nc("ALL")
    maps = [core_inputs(inputs, c) for c in range(8)]
    res = run_bass_kernel_spmd(nc, maps, core_ids=list(range(8)))
    out = np.zeros((4, S, D), np.float32)
    for c in range(8):
        b, p = divmod(c, 2)
        out[b][own_rows(p)] = np.asarray(res.results[c]["out_own"], np.float32)
    return out
```
